# Optimizing a Trainium2 kernel written in Bass

```python
import math
import jax, jax.numpy as jnp
from jax import lax
import numpy as np

D_MODEL = 2048
BATCH = 32
SEQ = 256
DEPTH = 4
DEC_BATCH = 8
DEC_SEQ = 4096
PAST_LEN = 256

GRID_W = 64
D_MIX = D_MODEL
H_A = 8
DK_A = 128
DV_A = 128
D_A = H_A * DV_A
CONV_K = 5
CHUNK = 64
H_B = 4
NOPE_B = 128
ROPE_B = 64
VD_B = 128
D_B = H_B * VD_B
Q_LORA = 384
KV_LORA = 256
Q_BLOCK = 128
ROPE_BASE = 10000.0
POOL_WINDOWS = (2, 4, 8, 16)
N_POOL = 4
D_C = D_MIX - D_A - D_B
GC = D_C // N_POOL
EPS = 1e-6
IN_SIZES = (D_A, D_A, D_A, D_A, 2 * H_A, 2 * H_A, Q_LORA, KV_LORA, ROPE_B, D_B, D_C, D_C)
IN_COLS = 4 * D_A + 4 * H_A + Q_LORA + KV_LORA + ROPE_B + D_B + 2 * D_C
F32 = jnp.float32

kernel_name = 'hybrid_deltanet_mla_pool_diffusion_step'


def rmsnorm(x, w):
    xf = x.astype(F32)
    y = xf * lax.rsqrt(jnp.mean(xf * xf, axis=-1, keepdims=True) + EPS)
    return (y * w.astype(F32)).astype(x.dtype)


def l2norm(x):
    xf = x.astype(F32)
    return xf * lax.rsqrt(jnp.sum(xf * xf, axis=-1, keepdims=True) + EPS)


def split_cols(h):
    return jnp.split(h, np.cumsum(IN_SIZES)[:-1].tolist(), axis=-1)


def centred_dwconv(x, w):
    pad = CONV_K // 2
    return lax.conv_general_dilated(
        x, w.astype(x.dtype)[:, None, :], window_strides=(1,), padding=[(pad, pad)],
        dimension_numbers=('NWC', 'WIO', 'NWC'), feature_group_count=x.shape[-1])


def rope2d_tables(T):
    rows = T // GRID_W
    row = jnp.repeat(jnp.arange(rows, dtype=F32), GRID_W)
    col = jnp.tile(jnp.arange(GRID_W, dtype=F32), rows)
    nf = ROPE_B // 4
    inv = ROPE_BASE ** (-jnp.arange(nf, dtype=F32) / nf)
    ang = jnp.stack([row[:, None] * inv, col[:, None] * inv], axis=1)
    return jnp.cos(ang), jnp.sin(ang)


def apply_rope2d(x, cos, sin):
    shp = x.shape
    xr = x.astype(F32).reshape(shp[:-1] + (2, 2, ROPE_B // 4))
    a, b = xr[..., 0, :], xr[..., 1, :]
    out = jnp.stack([a * cos - b * sin, a * sin + b * cos], axis=-2)
    return out.reshape(shp).astype(x.dtype)


def gated_delta_chunked(q, k, v, g, beta, s0):
    B, T, H, DK = q.shape
    DV = v.shape[-1]
    n = T // CHUNK

    def chunks(a):
        a = a.reshape((B, n, CHUNK, H) + a.shape[3:])
        return jnp.moveaxis(a, 3, 2).swapaxes(0, 1)

    q = chunks(q * DK ** -0.5)
    k = chunks(k)
    v = chunks(v)
    g = chunks(g)
    beta = chunks(beta)
    gc = jnp.cumsum(g, axis=-1)
    tri = jnp.tril(jnp.ones((CHUNK, CHUNK), bool))
    strict = jnp.tril(jnp.ones((CHUNK, CHUNK), bool), -1)
    decay = jnp.exp(jnp.where(tri, gc[..., :, None] - gc[..., None, :], -jnp.inf))
    kb = k * beta[..., None]
    lmat = jnp.where(strict, jnp.einsum('nbhid,nbhjd->nbhij', kb, k) * decay, 0.0)
    eye = jnp.eye(CHUNK, dtype=F32)
    rhs = jnp.concatenate([kb * jnp.exp(gc)[..., None], v * beta[..., None]], axis=-1)
    sol = lax.linalg.triangular_solve(lmat + eye, rhs, left_side=True, lower=True,
                                      unit_diagonal=True)
    w, u = sol[..., :DK], sol[..., DK:]
    attn = jnp.einsum('nbhid,nbhjd->nbhij', q, k) * decay
    qg = q * jnp.exp(gc)[..., None]
    kg = k * jnp.exp(gc[..., -1:] - gc)[..., None]
    glast = jnp.exp(gc[..., -1])

    def step(s, xs):
        w_i, u_i, attn_i, qg_i, kg_i, gl_i = xs
        v_new = u_i - jnp.einsum('bhcd,bhde->bhce', w_i, s)
        o = jnp.einsum('bhcd,bhde->bhce', qg_i, s) + jnp.einsum('bhij,bhje->bhie', attn_i, v_new)
        s = s * gl_i[..., None, None] + jnp.einsum('bhcd,bhce->bhde', kg_i, v_new)
        return s, o

    s_fin, o = lax.scan(step, s0, (w, u, attn, qg, kg, glast))
    o = jnp.moveaxis(o.swapaxes(0, 1), 2, 3).reshape(B, T, H, DV)
    return o, s_fin


def delta_branch(qa, ka, va, za, beta_raw, alpha_raw, conv_w, a_log, dt_bias, o_norm, s0_f, s0_b):
    B, T, _ = qa.shape
    qkv = jax.nn.silu(centred_dwconv(jnp.concatenate([qa, ka, va], axis=-1), conv_w))
    q, k, v = jnp.split(qkv, 3, axis=-1)
    q = l2norm(q.reshape(B, T, H_A, DK_A))
    k = l2norm(k.reshape(B, T, H_A, DK_A))
    v = v.reshape(B, T, H_A, DV_A).astype(F32)
    beta = jax.nn.sigmoid(beta_raw.astype(F32)).reshape(B, T, 2, H_A)
    g = -jnp.exp(a_log.astype(F32)) * jax.nn.softplus(
        alpha_raw.astype(F32).reshape(B, T, 2, H_A) + dt_bias.astype(F32))
    o_f, s_f = gated_delta_chunked(q, k, v, g[:, :, 0], beta[:, :, 0], s0_f.astype(F32))
    flip = lambda a: jnp.flip(a, axis=1)
    o_b, s_b = gated_delta_chunked(flip(q), flip(k), flip(v), flip(g[:, :, 1]),
                                   flip(beta[:, :, 1]), s0_b.astype(F32))
    o = rmsnorm(o_f + flip(o_b), o_norm) * jax.nn.silu(za.astype(F32).reshape(B, T, H_A, DV_A))
    return o.reshape(B, T, D_A).astype(qa.dtype), s_f, s_b


def mla_expand(ckv_n, w_ukv):
    B, S, _ = ckv_n.shape
    kv = (ckv_n @ w_ukv).reshape(B, S, H_B, NOPE_B + VD_B)
    return kv[..., :NOPE_B], kv[..., NOPE_B:]


def mla_attend(qn, qp, kn, kp, v):
    B, T = qn.shape[:2]
    nb = T // Q_BLOCK
    scale = (NOPE_B + ROPE_B) ** -0.5

    def blocks(a):
        return a.reshape((B, nb, Q_BLOCK) + a.shape[2:]).swapaxes(0, 1)

    def one_block(args):
        qn_i, qp_i = args
        s = jnp.einsum('bqhd,bkhd->bhqk', qn_i, kn) + jnp.einsum('bqhr,bkr->bhqk', qp_i, kp)
        p = jax.nn.softmax(s.astype(F32) * scale, axis=-1).astype(v.dtype)
        return jnp.einsum('bhqk,bkhd->bqhd', p, v)

    o = lax.map(one_block, (blocks(qn), blocks(qp)))
    return o.swapaxes(0, 1).reshape(B, T, H_B * VD_B)


def multiscale_pool(xc, w_pool, pool_scale):
    B, T, _ = xc.shape
    xf = xc.astype(F32)
    cs = jnp.concatenate([jnp.zeros((B, 1, D_C), F32), jnp.cumsum(xf, axis=1)], axis=1)
    t = jnp.arange(T)
    outs = []
    for gi, win in enumerate(POOL_WINDOWS):
        lo = win // 2
        hi = win - 1 - lo
        start = jnp.maximum(t - lo, 0)
        end = jnp.minimum(t + hi + 1, T)
        csg = cs[..., gi * GC:(gi + 1) * GC]
        mean = (csg[:, end] - csg[:, start]) / (end - start).astype(F32)[None, :, None]
        outs.append(mean - xf[..., gi * GC:(gi + 1) * GC])
    pooled = jnp.stack(outs, axis=2)
    y = jnp.einsum('btgc,gcd->btgd', pooled, w_pool.astype(F32)).reshape(B, T, D_C)
    return (y * pool_scale.astype(F32)).astype(xc.dtype)


def setup_inputs(seed: int = 0) -> dict:
    key = jax.random.key(seed)
    ks = jax.random.split(key, 26)

    def nrm(k, shape, s):
        return s * jax.random.normal(k, shape, F32)

    dt = jnp.exp(jax.random.uniform(ks[12], (DEPTH, 2, H_A), F32, math.log(1e-3), math.log(1e-1)))
    return {
        'x_prompt': nrm(ks[0], (BATCH, SEQ, D_MODEL), 1.0),
        'x_sample': nrm(ks[1], (DEC_BATCH, DEC_SEQ, D_MODEL), 1.0),
        'cache_ckv': nrm(ks[2], (DEC_BATCH, DEPTH, PAST_LEN, KV_LORA), 1.0),
        'cache_kpe': nrm(ks[3], (DEC_BATCH, DEPTH, PAST_LEN, ROPE_B), 1.0),
        'state_delta_fwd': nrm(ks[4], (DEC_BATCH, DEPTH, H_A, DK_A, DV_A), 0.1),
        'state_delta_bwd': nrm(ks[5], (DEC_BATCH, DEPTH, H_A, DK_A, DV_A), 0.1),
        'c': nrm(ks[6], (DEC_BATCH, D_MODEL), 1.0),
        'c_ctx': nrm(ks[7], (D_MODEL,), 1.0),
        'w_ada': nrm(ks[8], (DEPTH, D_MODEL, 3 * D_MODEL), 0.5 * D_MODEL ** -0.5),
        'b_ada': nrm(ks[9], (DEPTH, 3 * D_MODEL), 0.02),
        'norm_pre': 1.0 + nrm(ks[10], (DEPTH, D_MODEL), 0.02),
        'norm_post': 1.0 + nrm(ks[11], (DEPTH, D_MODEL), 0.02),
        'w_in': nrm(ks[13], (DEPTH, D_MODEL, IN_COLS), D_MODEL ** -0.5),
        'conv_w': nrm(ks[14], (DEPTH, CONV_K, 3 * D_A), CONV_K ** -0.5),
        'a_log': jnp.log(jax.random.uniform(ks[15], (DEPTH, 2, H_A), F32, 1.0, 16.0)),
        'dt_bias': dt + jnp.log(-jnp.expm1(-dt)),
        'o_norm_a': 1.0 + nrm(ks[16], (DEPTH, DV_A), 0.02),
        'q_norm': 1.0 + nrm(ks[17], (DEPTH, Q_LORA), 0.02),
        'kv_norm': 1.0 + nrm(ks[18], (DEPTH, KV_LORA), 0.02),
        'w_uq': nrm(ks[19], (DEPTH, Q_LORA, H_B * (NOPE_B + ROPE_B)), Q_LORA ** -0.5),
        'w_ukv': nrm(ks[20], (DEPTH, KV_LORA, H_B * (NOPE_B + VD_B)), KV_LORA ** -0.5),
        'w_pool': nrm(ks[21], (DEPTH, N_POOL, GC, GC), GC ** -0.5),
        'pool_scale': 1.0 + nrm(ks[22], (DEPTH, D_C), 0.1),
        'w_out': nrm(ks[23], (DEPTH, D_MIX, D_MODEL), D_MIX ** -0.5),
    }


def reference(x_prompt, x_sample, cache_ckv, cache_kpe, state_delta_fwd, state_delta_bwd, c, c_ctx,
              w_ada, b_ada, norm_pre, norm_post, w_in, conv_w, a_log, dt_bias, o_norm_a, q_norm,
              kv_norm, w_uq, w_ukv, w_pool, pool_scale, w_out):

    def layer(x, mod, l, ctx):
        B, T, _ = x.shape
        shift, scale, gate = jnp.split(mod, 3, axis=-1)
        h = rmsnorm(x, norm_pre[l]) * (1.0 + scale) + shift
        (qa, ka, va, za, beta_raw, alpha_raw, cq, ckv, kpe, zb, xc, zc) = split_cols(h @ w_in[l])
        if ctx is None:
            s0_f = jnp.zeros((B, H_A, DK_A, DV_A), F32)
            s0_b = jnp.zeros((B, H_A, DK_A, DV_A), F32)
        else:
            ckv_c, kpe_c, s0_f, s0_b = ctx
        oa, s_f, s_b = delta_branch(qa, ka, va, za, beta_raw, alpha_raw, conv_w[l], a_log[l],
                                    dt_bias[l], o_norm_a[l], s0_f, s0_b)
        qfull = (rmsnorm(cq, q_norm[l]) @ w_uq[l]).reshape(B, T, H_B, NOPE_B + ROPE_B)
        qn, qp = qfull[..., :NOPE_B], qfull[..., NOPE_B:]
        ckv_n = rmsnorm(ckv, kv_norm[l])
        kn, vb = mla_expand(ckv_n, w_ukv[l])
        kp = kpe
        if ctx is not None:
            cos, sin = rope2d_tables(T)
            qp = apply_rope2d(qp, cos[:, None], sin[:, None])
            kp = apply_rope2d(kpe, cos, sin)
            kn_c, vb_c = mla_expand(ckv_c, w_ukv[l])
            kn = jnp.concatenate([kn_c, kn], axis=1)
            vb = jnp.concatenate([vb_c, vb], axis=1)
            kp = jnp.concatenate([kpe_c, kp], axis=1)
        ob = mla_attend(qn, qp, kn, kp, vb) * jax.nn.silu(zb)
        oc = multiscale_pool(xc, w_pool[l], pool_scale[l]) * jax.nn.silu(zc)
        y = jnp.concatenate([oa, ob, oc], axis=-1) @ w_out[l]
        x = x + gate * rmsnorm(y, norm_post[l])
        return x, ckv_n, kpe, s_f, s_b

    xp = x_prompt
    xs = x_sample
    ckvs, kpes, sfs, sbs = [], [], [], []
    for l in range(DEPTH):
        mod_ctx = (jax.nn.silu(c_ctx) @ w_ada[l] + b_ada[l])[None, None, :]
        mod_lat = (jax.nn.silu(c) @ w_ada[l] + b_ada[l])[:, None, :]
        xp, ckv_n, kpe, s_f, s_b = layer(xp, mod_ctx, l, None)
        ckvs.append(ckv_n)
        kpes.append(kpe)
        sfs.append(s_f)
        sbs.append(s_b)
        xs = layer(xs, mod_lat, l, (cache_ckv[:, l], cache_kpe[:, l],
                                    state_delta_fwd[:, l], state_delta_bwd[:, l]))[0]
    new_ckv = jnp.stack(ckvs, axis=1)
    new_kpe = jnp.stack(kpes, axis=1)
    new_state_fwd = jnp.stack(sfs, axis=1)
    new_state_bwd = jnp.stack(sbs, axis=1)
    return (xp, xs, new_ckv, new_kpe, new_state_fwd, new_state_bwd)
```

```python
import numpy as np
import concourse.bass as bass
import concourse.mybir as mybir
from concourse.alu_op_type import AluOpType as ALU
from contextlib import ExitStack, contextmanager

F32 = mybir.dt.float32
BF16 = mybir.dt.bfloat16
F32R = mybir.dt.float32r
AF = mybir.ActivationFunctionType
AX = mybir.AxisListType


class TV:
    __slots__ = ("obj", "ap")

    def __init__(self, obj, ap):
        self.obj = obj
        self.ap = ap

    def __getitem__(self, idx):
        return TV(self.obj, self.ap[idx])

    def bitcast(self, dt):
        return TV(self.obj, self.ap.bitcast(dt))

    def bcast(self, shape):
        return TV(self.obj, self.ap.broadcast_to(shape))

    def pbc(self, n=128):
        return TV(self.obj, self.ap.partition_broadcast(n))

    def unsq(self, ax):
        return TV(self.obj, self.ap.unsqueeze(ax))

    def rr(self, pat, **kw):
        return TV(self.obj, self.ap.rearrange(pat, **kw))


class Trk:
    def __init__(self, name, kind):
        self.name = name
        self.kind = kind
        self.lw = None
        self.rd = {}
        self.sem = None
        self.dma_last = 0
        self.dma_kind = None
        self.dw = {}
        self.dr = {}


class Tile(Trk):
    def __init__(self, name, kind, handle):
        super().__init__(name, kind)
        self.h = handle
        self.bank = self
        self.acc = {}

    def __getitem__(self, idx):
        return TV(self, self.h[idx])

    def sub(self, name):
        t = Tile(name, self.kind, self.h)
        t.bank = self.bank
        return t


class Dram(Trk):
    def __init__(self, name, ap):
        super().__init__(name, "dram")
        self.apx = ap

    def __getitem__(self, idx):
        return TV(self, self.apx[idx])

    def v(self, ap):
        return TV(self, ap)


class DSem:
    def __init__(self, obj):
        self.obj = obj
        self.count = 0


class Ctx:
    NDSEM = 96

    def __init__(self, nc, es):
        self.nc = nc
        self.es = es
        self.eng = {"pe": nc.tensor, "act": nc.scalar, "dve": nc.vector, "pool": nc.gpsimd, "sp": nc.sync}
        self.sem = {}
        self.seq = {}
        for k in self.eng:
            self.sem[k] = es.enter_context(nc.semaphore("s_" + k))
            self.seq[k] = 0
        self.waited = {k: {} for k in self.eng}
        self.free_dsems = [DSem(es.enter_context(nc.semaphore("d%d" % i))) for i in range(self.NDSEM)]
        self.dma_tiles = []
        self.scope_tiles = [[]]
        self.n_instr = 0
        self.n_wait = 0
        self.drams = []

    def mark(self, label):
        if not hasattr(self, "marks"):
            self.marks = []
        self.marks.append((label, dict(self.seq)))

    def sb(self, name, shape, dt):
        self.uid = getattr(self, "uid", 0) + 1
        name = "sb%d_%s" % (self.uid, name)
        h = self.es.enter_context(self.nc.sbuf_tensor(name, list(shape), dt))
        return Tile(name, "sb", h)

    def ps(self, name, shape, dt):
        h = self.es.enter_context(self.nc.psum_tensor("pp_" + name, list(shape), dt))
        return Tile(name, "ps", h)

    def dram(self, name, shape, dt, kind=None):
        if kind is None:
            t = self.nc.dram_tensor(name, list(shape), dt)
        else:
            t = self.nc.dram_tensor(name, list(shape), dt, kind=kind)
        d = Dram(name, t.ap())
        self.drams.append(d)
        return d

    def _wait(self, e, semkey, semobj, val):
        w = self.waited[e]
        if w.get(semkey, 0) >= val:
            return
        self.eng[e].wait_ge(semobj, val)
        w[semkey] = val
        self.n_wait += 1

    def _wait_eng(self, e, p, seq):
        if p == "pe" and e == "pe":
            return
        self._wait(e, p, self.sem[p], seq)

    def _sync_dma(self, e, t):
        if t.sem is not None and t.dma_last > 0:
            self._wait(e, id(t.sem), t.sem.obj, 16 * t.dma_last)

    def _deps(self, e, reads, writes, same_engine_war=False):
        for t in reads:
            if t.lw is not None:
                self._wait_eng(e, *t.lw)
            self._sync_dma(e, t)
        for t in writes:
            if t.lw is not None and t.lw[0] != e:
                self._wait_eng(e, *t.lw)
            for p, s in t.rd.items():
                if p != e:
                    self._wait_eng(e, p, s)
            self._sync_dma(e, t)

    def op(self, e, fn, reads, writes):
        reads = [r.obj if isinstance(r, TV) else r for r in reads]
        writes = [w.obj if isinstance(w, TV) else w for w in writes]
        self._deps(e, reads, writes)
        banks = []
        for t in reads + writes:
            if t.kind == "ps" and t.bank not in banks:
                banks.append(t.bank)
                for p, sq in t.bank.acc.items():
                    if p != e:
                        self._wait_eng(e, p, sq)
        ins = fn()
        self.seq[e] += 1
        ins.then_inc(self.sem[e], 1)
        s = self.seq[e]
        for b in banks:
            b.acc[e] = s
        for t in reads:
            if t.rd.get(e, 0) < s:
                t.rd[e] = s
        for t in writes:
            t.lw = (e, s)
            t.rd = {}
        self.n_instr += 1
        return ins

    def _tsem(self, t):
        if t.sem is None:
            t.sem = self.free_dsems.pop()
            self.dma_tiles.append(t)
            self.scope_tiles[-1].append(t)
        return t.sem

    @contextmanager
    def scope(self):
        old = self.es
        self.scope_tiles.append([])
        with ExitStack() as es2:
            self.es = es2
            try:
                yield
            finally:
                self.barrier()
                self.es = old
                for t in self.scope_tiles.pop():
                    self.free_dsems.append(t.sem)
                    self.dma_tiles.remove(t)
                    t.sem = None

    def dma(self, q, out, in_, **kw):
        e = q
        src, dst = in_.obj, out.obj
        if dst.kind == "sb":
            t = dst
            kind = "load"
        elif src.kind == "sb":
            t = src
            kind = "store"
        else:
            raise ValueError("dram->dram dma unsupported here")
        sem = self._tsem(t)
        if src.kind == "sb":
            if src.lw is not None:
                self._wait_eng(e, *src.lw)
            if src is not t or (t.dma_kind is not None and t.dma_kind != kind):
                self._sync_dma(e, src)
        if dst.kind == "sb":
            if dst.lw is not None:
                self._wait_eng(e, *dst.lw)
            for p, s in dst.rd.items():
                self._wait_eng(e, p, s)
            if dst is not t or (t.dma_kind is not None and t.dma_kind != kind):
                self._sync_dma(e, dst)
        if src.kind == "dram":
            for k, (so, cnt) in src.dw.items():
                self._wait(e, k, so, 16 * cnt)
        if dst.kind == "dram":
            for k, (so, cnt) in dst.dr.items():
                self._wait(e, k, so, 16 * cnt)
        ins = self.eng[e].dma_start(out=out.ap, in_=in_.ap, **kw)
        ins.then_inc(sem.obj, 16)
        sem.count += 1
        t.dma_last = sem.count
        t.dma_kind = kind
        if src.kind == "dram":
            src.dr[id(sem)] = (sem.obj, sem.count)
        if dst.kind == "dram":
            dst.dw[id(sem)] = (sem.obj, sem.count)
        if src.kind == "sb" and dst.kind == "sb":
            pass
        self.n_instr += 1
        return ins

    def barrier(self):
        for t in self.dma_tiles:
            self._sync_dma("sp", t)
        names = list(self.eng)
        for e in names:
            for p in names:
                if p != e and self.seq[p] > 0 and not (p == "sp"):
                    self._wait(e, p, self.sem[p], self.seq[p])
        ins = self.nc.sync.nop()
        self.seq["sp"] += 1
        ins.then_inc(self.sem["sp"], 1)
        for e in names:
            if e != "sp":
                self._wait(e, "sp", self.sem["sp"], self.seq["sp"])
        for d in self.drams:
            d.dw = {}
            d.dr = {}

    def final_wait(self):
        for t in self.dma_tiles:
            self._sync_dma("sp", t)

    def mm(self, out, lhsT, rhs, start=True, stop=True):
        return self.op("pe", lambda: self.nc.tensor.matmul(out.ap, lhsT=lhsT.ap, rhs=rhs.ap, start=start, stop=stop),
                       [lhsT, rhs] + ([] if start else [out]), [out])

    def tr(self, out, in_, ident):
        return self.op("pe", lambda: self.nc.tensor.transpose(out.ap, in_.ap, ident.ap), [in_, ident], [out])

    def act(self, out, in_, func, bias=None, scale=None, accum=None, eng="act"):
        kw = {}
        rd = [in_]
        wr = [out]
        if bias is not None:
            if isinstance(bias, TV):
                kw["bias"] = bias.ap
                rd.append(bias)
            else:
                kw["bias"] = bias
        if scale is not None:
            if isinstance(scale, TV):
                kw["scale"] = scale.ap
                rd.append(scale)
            else:
                kw["scale"] = scale
        if accum is not None:
            kw["accum_out"] = accum.ap
            wr.append(accum)
        return self.op("act", lambda: self.nc.scalar.activation(out=out.ap, in_=in_.ap, func=func, **kw), rd, wr)

    def _e(self, eng):
        return self.eng[eng]

    def tt(self, out, a, b, op, eng="dve"):
        return self.op(eng, lambda: self._e(eng).tensor_tensor(out=out.ap, in0=a.ap, in1=b.ap, op=op), [a, b], [out])

    def ts(self, out, a, s1, op0, s2=None, op1=None, eng="dve", accum=None):
        rd = [a]
        wr = [out]
        v1 = s1.ap if isinstance(s1, TV) else s1
        v2 = s2.ap if isinstance(s2, TV) else s2
        if isinstance(s1, TV):
            rd.append(s1)
        if isinstance(s2, TV):
            rd.append(s2)
        kw = {}
        if op1 is not None:
            kw["op1"] = op1
        if accum is not None:
            kw["accum_out"] = accum.ap
            wr.append(accum)
        return self.op(eng, lambda: self._e(eng).tensor_scalar(out=out.ap, in0=a.ap, scalar1=v1, scalar2=v2, op0=op0, **kw), rd, wr)

    def stt(self, out, a, s, b, op0, op1, eng="dve"):
        rd = [a, b]
        v = s.ap if isinstance(s, TV) else s
        if isinstance(s, TV):
            rd.append(s)
        return self.op(eng, lambda: self._e(eng).scalar_tensor_tensor(out=out.ap, in0=a.ap, scalar=v, in1=b.ap, op0=op0, op1=op1), rd, [out])

    def copy(self, out, in_, eng="dve"):
        if eng == "act":
            return self.op("act", lambda: self.nc.scalar.copy(out=out.ap, in_=in_.ap), [in_], [out])
        return self.op(eng, lambda: self._e(eng).tensor_copy(out=out.ap, in_=in_.ap), [in_], [out])

    def recip(self, out, in_):
        return self.op("dve", lambda: self.nc.vector.reciprocal(out=out.ap, in_=in_.ap), [in_], [out])

    def memset(self, out, val, eng="dve"):
        return self.op(eng, lambda: self._e(eng).memset(out.ap, val), [], [out])

    def reduce(self, out, in_, op, axis=AX.X):
        return self.op("dve", lambda: self.nc.vector.tensor_reduce(out=out.ap, in_=in_.ap, axis=axis, op=op), [in_], [out])


D = 2048
DEPTH = 4
NCORE = 8
TP = 256
NP = 4
TS = 4096
PAST = 256
NTOK = NP * TP + TS
H_A = 8
EPS = 1e-6
NFM = 50
NTM = 352
CH_Q, CH_K, CH_V, CH_ZA, CH_CQ, CH_CKV, CH_ZB, CH_XC, CH_ZC, CH_KPE = 0, 8, 16, 24, 32, 35, 37, 41, 45, 49
PASSES = [(0, 13), (13, 26), (26, 38), (38, 50)]


def _swap_perm():
    p = np.arange(64).reshape(2, 2, 16)
    return p[:, ::-1, :].reshape(64)


def host_prep(inp):
    f32 = np.float32
    w_in = inp["w_in"]
    o = np.cumsum([0, 1024, 1024, 1024, 1024, 16, 16, 384, 256, 64, 512, 512, 512])
    (oq, ok, ov, oz, ob, oa, ocq, ockv, okpe, ozb, oxc, ozc, oend) = o
    sw = _swap_perm()
    cols = np.concatenate([
        np.arange(oq, oq + 4096),
        np.arange(ocq, ocq + 384),
        np.arange(ockv, ockv + 256),
        np.arange(ozb, ozb + 512),
        np.arange(oxc, oxc + 512),
        np.arange(ozc, ozc + 512),
        np.arange(okpe, okpe + 64), okpe + sw,
    ])
    assert cols.size == NFM * 128
    w_in_fm = np.ascontiguousarray(w_in[:, :, cols])
    cols_tm = np.concatenate([np.arange(ob, ob + 32), np.arange(ockv, ockv + 256), np.arange(okpe, okpe + 64)])
    w_in_tm = np.ascontiguousarray(w_in[:, :, cols_tm])
    wq = inp["w_uq"].reshape(DEPTH, 384, 4, 192)
    w_uq_ext = np.ascontiguousarray(np.concatenate([wq[..., :128], wq[..., 128:], wq[..., 128:][..., sw]], axis=-1).reshape(DEPTH, 384, 1024))
    wkv = inp["w_ukv"].reshape(DEPTH, 256, 4, 256)
    w_ukv_ext = np.ascontiguousarray(np.concatenate([wkv[..., :128].reshape(DEPTH, 256, 512), wkv[..., 128:].reshape(DEPTH, 256, 512)], axis=-1))
    convT = np.ascontiguousarray(inp["conv_w"].reshape(DEPTH, 5, 24, 128).transpose(3, 0, 2, 1))
    colp = lambda a, n: np.ascontiguousarray(a.reshape(DEPTH, n, 128).transpose(2, 0, 1))
    idx = np.arange(128)
    same = (idx[:, None] // 64) == (idx[None, :] // 64)
    vf = same & (idx[None, :] >= idx[:, None])
    vb = same & (idx[None, :] <= idx[:, None])
    masks = np.zeros((128, 8, 128), f32)
    masks[:, 0] = vf
    masks[:, 1] = vb
    masks[:, 2] = same & (idx[None, :] > idx[:, None])
    masks[:, 3] = same & (idx[None, :] < idx[:, None])
    masks[:, 4] = (vf.astype(f32) - 1.0) * 30000.0
    masks[:, 5] = (vb.astype(f32) - 1.0) * 30000.0
    masks[:, 6] = same
    masks[:, 7] = np.eye(128)
    sel = np.zeros((128, 2, 128), f32)
    sel[:64, 0] = 1
    sel[64:, 1] = 1
    T = TS
    rows = T // 64
    row = np.repeat(np.arange(rows, dtype=f32), 64)
    col = np.tile(np.arange(64, dtype=f32), rows)
    nf = 16
    inv = (10000.0 ** (-np.arange(nf, dtype=f32) / nf)).astype(f32)
    ang = np.stack([row[:, None] * inv, col[:, None] * inv], axis=1)
    cos, sin = np.cos(ang).astype(f32), np.sin(ang).astype(f32)
    COS = np.stack([cos, cos], axis=2).reshape(T, 64).T
    SIN = np.stack([-sin, sin], axis=2).reshape(T, 64).T
    rope = np.ascontiguousarray(np.stack([COS, SIN], axis=1)).astype(f32)
    corr = np.ones((4, 16), f32)
    for gi, win in enumerate((2, 4, 8, 16)):
        lo = win // 2
        hi = win - 1 - lo
        for t in range(8):
            corr[gi, t] = win / (min(t, lo) + hi + 1)
            corr[gi, 8 + t] = win / (lo + 1 + min(hi, 7 - t))
    shared = {
        "w_ada": inp["w_ada"], "b_ada": inp["b_ada"], "norm_pre": inp["norm_pre"], "norm_post": inp["norm_post"],
        "w_in_fm": w_in_fm, "w_in_tm": w_in_tm, "convT": convT,
        "a_log": np.ascontiguousarray(inp["a_log"].reshape(DEPTH, 16)), "dt_bias": np.ascontiguousarray(inp["dt_bias"].reshape(DEPTH, 16)),
        "onormT": np.ascontiguousarray(inp["o_norm_a"].T), "qnormT": colp(inp["q_norm"], 3), "kvnormT": colp(inp["kv_norm"], 2),
        "kv_norm": inp["kv_norm"], "pscaleT": colp(inp["pool_scale"], 4),
        "w_uq": w_uq_ext, "w_ukv": w_ukv_ext, "w_pool": inp["w_pool"], "w_out": inp["w_out"],
        "masks": masks, "sel": sel, "rope": rope, "corr": corr,
    }
    per = []
    for c in range(NCORE):
        c2 = np.stack([inp["c_ctx"], inp["c"][c]], axis=0)
        cT = np.ascontiguousarray(c2.reshape(2, 16, 128).transpose(2, 1, 0))
        per.append({
            "xp": np.ascontiguousarray(inp["x_prompt"][c * NP:(c + 1) * NP].reshape(NP * TP, D)),
            "xs": np.ascontiguousarray(inp["x_sample"][c]),
            "cckv": np.ascontiguousarray(inp["cache_ckv"][c]), "ckpe": np.ascontiguousarray(inp["cache_kpe"][c]),
            "s0f": np.ascontiguousarray(inp["state_delta_fwd"][c]), "s0b": np.ascontiguousarray(inp["state_delta_bwd"][c]),
            "cT": cT,
        })
    return shared, per


def build(cfg):
    depth = cfg.get("depth", DEPTH)
    stop_after = cfg.get("stop_after", None)
    nc = bass.Bass("TRN2", target_bir_lowering=False)
    es = ExitStack()
    with es:
        cx = Ctx(nc, es)
        _program(cx, cfg, depth, stop_after)
        cx.final_wait()
        print("instr", cx.n_instr, "waits", cx.n_wait, "seq", cx.seq)
        build.marks = getattr(cx, "marks", [])
    return nc


def _program(cx, cfg, depth, stop_after):
    nc = cx.nc
    IN = lambda n, s, dt=F32: cx.dram(n, s, dt, kind="ExternalInput")
    OUT = lambda n, s, dt=F32: cx.dram(n, s, dt, kind="ExternalOutput")
    xp = IN("xp", [NP * TP, D]); xs = IN("xs", [TS, D])
    cckv = IN("cckv", [DEPTH, PAST, 256]); ckpe = IN("ckpe", [DEPTH, PAST, 64])
    s0f = IN("s0f", [DEPTH, 8, 128, 128]); s0b = IN("s0b", [DEPTH, 8, 128, 128])
    cT = IN("cT", [128, 16, 2])
    w_ada = IN("w_ada", [DEPTH, D, 3 * D]); b_ada = IN("b_ada", [DEPTH, 3 * D])
    norm_pre = IN("norm_pre", [DEPTH, D]); norm_post = IN("norm_post", [DEPTH, D])
    w_in_fm = IN("w_in_fm", [DEPTH, D, NFM * 128]); w_in_tm = IN("w_in_tm", [DEPTH, D, NTM])
    convT = IN("convT", [128, DEPTH, 24, 5])
    a_log = IN("a_log", [DEPTH, 16]); dt_bias = IN("dt_bias", [DEPTH, 16])
    onormT = IN("onormT", [128, DEPTH]); qnormT = IN("qnormT", [128, DEPTH, 3]); kvnormT = IN("kvnormT", [128, DEPTH, 2])
    kv_norm = IN("kv_norm", [DEPTH, 256]); pscaleT = IN("pscaleT", [128, DEPTH, 4])
    w_uq = IN("w_uq", [DEPTH, 384, 1024]); w_ukv = IN("w_ukv", [DEPTH, 256, 1024])
    w_pool = IN("w_pool", [DEPTH, 4, 128, 128]); w_out = IN("w_out", [DEPTH, D, D])
    masks_d = IN("masks", [128, 8, 128]); sel_d = IN("sel", [128, 2, 128]); rope_d = IN("rope", [64, 2, TS]); corr_d = IN("corr", [4, 16])
    yp = OUT("yp", [NP * TP, D]); ys = OUT("ys", [TS, D])
    nckv = OUT("nckv", [NP, DEPTH, TP, 256]); nkpe = OUT("nkpe", [NP, DEPTH, TP, 64])
    nsf = OUT("nsf", [NP, DEPTH, 8, 128, 128]); nsb = OUT("nsb", [NP, DEPTH, 8, 128, 128])
    dbg = cfg.get("dbg", ())
    SCR = lambda n, s, dt: cx.dram(n, s, dt, kind=("ExternalOutput" if n in dbg else None))
    hT_d = SCR("hT_d", [D, NTOK], BF16)
    projT_d = SCR("projT_d", [NFM * 128, NTOK], BF16)
    ab_d = SCR("ab_d", [NTOK, 32], F32)
    mixT_d = SCR("mixT_d", [D, NTOK], BF16)
    qk_d = SCR("qk_d", [8, 2, 128, NTOK], BF16)
    ktm_d = SCR("ktm_d", [NTOK, 1024], BF16)
    vtm_d = SCR("vtm_d", [NTOK, 1024], BF16)
    of_d = SCR("of_d", [1024, NTOK], BF16)
    ob_d = SCR("ob_d", [1024, NTOK], BF16)

    ident_f = cx.sb("ident_f", [128, 128], F32)
    ident_b = cx.sb("ident_b", [128, 128], BF16)
    ones_f = cx.sb("ones_f", [128, 128], F32)
    ones_b = cx.sb("ones_b", [128, 128], BF16)
    masks = cx.sb("masks", [128, 8, 128], F32)
    scT = cx.sb("scT", [128, 16, 2], F32)
    cx.dma("sp", masks[:], masks_d[:])
    cx.dma("sp", scT[:], cT[:])
    cx.copy(ident_f[:], masks[:, 7, :])
    cx.copy(ident_b[:], masks[:, 7, :])
    cx.memset(ones_f[:], 1.0)
    cx.memset(ones_b[:], 1.0)
    cx.act(scT[:], scT[:], AF.Silu)
    PS = [cx.ps("ps%d" % i, [128, 512], F32) for i in range(8)]
    modbc = cx.sb("modbc", [128, 2, 3, D], BF16)

    groups = []
    if cfg.get("do_prompt", True):
        groups.append(dict(name="p", r=0, tok0=0, nseq=NP, T=TP, ctx=False))
    if cfg.get("do_sample", True):
        groups.append(dict(name="s", r=1, tok0=NP * TP, nseq=1, T=TS, ctx=True))

    for l in range(depth):
        last = (l == depth - 1)
        cx.mark("L%d Phase 0" % l)
        with cx.scope():
            lb = cx.sb("lb", [128, 2, 16, 128], F32)
            for r in range(2):
                for k in range(16):
                    cx.ts(lb[:, r, k, :], ones_f[:], scT[:, k, r:r + 1], ALU.mult)
            wA = [cx.sb("wA%d" % i, [128, 16, 512], F32) for i in range(2)]
            bb = [cx.sb("bb%d" % i, [128, 512], F32) for i in range(2)]
            nb = [cx.sb("nb%d" % i, [128, 512], F32) for i in range(2)]
            tmp = cx.sb("p0tmp", [128, 512], F32)
            for nt in range(12):
                part, c0 = nt // 4, (nt % 4) * 512
                w_, b_, n_ = wA[nt % 2], bb[nt % 2], nb[nt % 2]
                cx.dma("sp", w_[:], w_ada[l, :, nt * 512:(nt + 1) * 512].rr("(k p) n -> p k n", p=128))
                cx.dma("sp", b_[:], b_ada[l, nt * 512:(nt + 1) * 512].pbc())
                if part == 1:
                    cx.dma("sp", n_[:], norm_pre[l, c0:c0 + 512].pbc())
                elif part == 2:
                    cx.dma("sp", n_[:], norm_post[l, c0:c0 + 512].pbc())
                for r in range(2):
                    ps = PS[(nt * 2 + r) % 4]
                    for k in range(16):
                        cx.mm(ps[:], lb[:, r, k, :], w_[:, k, :], start=(k == 0), stop=(k == 15))
                    if part == 0:
                        cx.tt(modbc[:, r, 1, c0:c0 + 512], ps[:], b_[:], ALU.add)
                    elif part == 1:
                        cx.stt(tmp[:], ps[:], 1.0, b_[:], ALU.add, ALU.add)
                        cx.tt(modbc[:, r, 0, c0:c0 + 512], tmp[:], n_[:], ALU.mult)
                    else:
                        cx.tt(tmp[:], ps[:], b_[:], ALU.add)
                        cx.tt(modbc[:, r, 2, c0:c0 + 512], tmp[:], n_[:], ALU.mult)
        if stop_after == "p0":
            continue
        cx.mark("L%d Phase 1a" % l)
        with cx.scope():
            xt = [cx.sb("xt%d" % i, [128, D], F32) for i in range(2)]
            junk = cx.sb("junk", [128, D], BF16)
            t1 = cx.sb("t1", [128, D], F32)
            hb = [cx.sb("hb%d" % i, [128, D], BF16) for i in range(2)]
            ss = cx.sb("ss", [128, 4], F32)
            hTt = [cx.sb("hTt%d" % i, [128, 16, 512], BF16) for i in range(2)]
            PSB = [PS[6 + i][:].bitcast(BF16).rr("p (k t) -> p k t", k=8) for i in range(2)]
            for g in groups:
                xin = (xp if g["name"] == "p" else xs) if l == 0 else (yp if g["name"] == "p" else ys)
                ntile = g["nseq"] * g["T"] // 128
                for ti in range(ntile):
                    x_ = xt[ti % 2]
                    h_ = hb[ti % 2]
                    hT_ = hTt[(ti // 4) % 2]
                    cx.dma("sp", x_[:], xin[ti * 128:(ti + 1) * 128, :])
                    cx.act(junk[:], x_[:], AF.Square, accum=ss[:, 0:1])
                    cx.act(ss[:, 1:2], ss[:, 0:1], AF.Ln, scale=1.0 / D, bias=EPS)
                    cx.act(ss[:, 2:3], ss[:, 1:2], AF.Exp, scale=-0.5)
                    cx.stt(t1[:], x_[:], ss[:, 2:3], modbc[:, g["r"], 0, :], ALU.mult, ALU.mult)
                    cx.tt(h_[:], t1[:], modbc[:, g["r"], 1, :], ALU.add)
                    for half in range(2):
                        pb = PSB[(ti * 2 + half) % 2]
                        for kk in range(8):
                            k = half * 8 + kk
                            cx.tr(pb[:, kk, :], h_[:, k * 128:(k + 1) * 128], ident_b[:])
                        cx.copy(hT_[:, half * 8:(half + 1) * 8, (ti % 4) * 128:(ti % 4 + 1) * 128], pb[:], eng=("act" if half else "dve"))
                    if ti % 4 == 3 or ti == ntile - 1:
                        n = (ti % 4 + 1) * 128
                        t0 = g["tok0"] + (ti // 4) * 512
                        cx.dma("pool", hT_d[:, t0:t0 + n].rr("(k p) t -> p k t", p=128), hT_[:, :, 0:n])
        if stop_after == "p1a":
            continue
        cx.mark("L%d Phase 1b" % l)
        with cx.scope():
            WMAX = max(ch - cl for cl, ch in PASSES) * 128
            wPs = [cx.sb("wP%d" % i, [128, 16, WMAX], BF16) for i in range(2)]
            wT = cx.sb("wT", [128, 16, NTM], BF16)
            kvn_bc = cx.sb("kvn_bc", [128, 256], F32)
            tmo = [cx.sb("tmo%d" % i, [128, NTM], F32) for i in range(2)]
            cko = [cx.sb("cko%d" % i, [128, 256], F32) for i in range(2)]
            st = cx.sb("st1b", [128, 4], F32)
            junk2 = cx.sb("junk2", [128, 256], BF16)
            hTt = [cx.sb("hTt%d" % i, [128, 16, 512], BF16) for i in range(2)]
            stage = [cx.sb("stage%d" % i, [128, 4, 512], BF16) for i in range(2)]

            def load_w(pi):
                cl, ch = PASSES[pi]
                for k in range(16):
                    cx.dma("pool", wPs[pi % 2][:, k, 0:(ch - cl) * 128], w_in_fm[l, k * 128:(k + 1) * 128, cl * 128:ch * 128])
            load_w(0)
            for k in range(16):
                cx.dma("pool", wT[:, k, :], w_in_tm[l, k * 128:(k + 1) * 128, :])
            cx.dma("sp", kvn_bc[:], kv_norm[l, :].pbc())
            ev = 0
            sti = 0
            hti = 0
            for pi, (c_lo, c_hi) in enumerate(PASSES):
                wP = wPs[pi % 2]
                if pi + 1 < len(PASSES):
                    load_w(pi + 1)
                for g in groups:
                    ntt = g["nseq"] * g["T"] // 512
                    for tt in range(ntt):
                        t0 = g["tok0"] + tt * 512
                        hT_ = hTt[hti % 2]
                        hti += 1
                        cx.dma("sp", hT_[:], hT_d[:, t0:t0 + 512].rr("(k p) t -> p k t", p=128))
                        if pi == 0:
                            for s_ in range(4):
                                ps = PS[4 + s_ % 2]
                                for k in range(16):
                                    cx.mm(ps[:, 0:NTM], hT_[:, k, s_ * 128:(s_ + 1) * 128], wT[:, k, :], start=(k == 0), stop=(k == 15))
                                o_ = tmo[s_ % 2]
                                cx.copy(o_[:], ps[:, 0:NTM], eng="act")
                                tk = t0 + s_ * 128
                                cx.dma("pool", ab_d[tk:tk + 128, :], o_[:, 0:32])
                                if not g["ctx"]:
                                    b_, tp = (tk - g["tok0"]) // TP, (tk - g["tok0"]) % TP
                                    cx.dma("pool", nkpe[b_, l, tp:tp + 128, :], o_[:, 288:352])
                                    ck = cko[s_ % 2]
                                    cx.act(junk2[:], o_[:, 32:288], AF.Square, accum=st[:, 0:1])
                                    cx.act(st[:, 1:2], st[:, 0:1], AF.Ln, scale=1.0 / 256, bias=EPS)
                                    cx.act(st[:, 2:3], st[:, 1:2], AF.Exp, scale=-0.5)
                                    cx.stt(ck[:], o_[:, 32:288], st[:, 2:3], kvn_bc[:], ALU.mult, ALU.mult)
                                    cx.dma("pool", nckv[b_, l, tp:tp + 128, :], ck[:])
                        for c in range(c_lo, c_hi):
                            ci = c - c_lo
                            ps = PS[ci % 4]
                            for k in range(16):
                                cx.mm(ps[:], wP[:, k, ci * 128:(ci + 1) * 128], hT_[:, k, :], start=(k == 0), stop=(k == 15))
                            sg = stage[sti % 2]
                            cx.copy(sg[:, ci % 4, :], ps[:], eng=("act" if ev % 2 else "dve"))
                            ev += 1
                            if ci % 4 == 3 or c == c_hi - 1:
                                n = ci % 4 + 1
                                cb = c - n + 1
                                cx.dma("pool", projT_d[cb * 128:(cb + n) * 128, t0:t0 + 512].rr("(c p) t -> p c t", p=128), sg[:, 0:n, :])
                                sti += 1
        if stop_after == "p1b":
            continue
        cx.mark("L%d Phase 2a" % l)
        with cx.scope():
            cw = cx.sb("cw", [128, 24, 5], F32)
            cx.dma("sp", cw[:], convT[:, l, :, :])
            dg = cx.sb("dg", [128, 24, 5, 128], BF16)
            for c in range(24):
                for k in range(5):
                    cx.ts(dg[:, c, k, :], ident_f[:], cw[:, c, k:k + 1], ALU.mult)
            PSB = [PS[6 + i][:].bitcast(BF16).rr("p (k t) -> p k t", k=8) for i in range(2)]
            xin = [cx.sb("xin%d" % i, [128, 516], BF16) for i in range(3)]
            sil = cx.sb("sil", [128, 16, 512], F32)
            silv = cx.sb("silv", [128, 8, 512], BF16)
            sq = [cx.sb("sq%d" % i, [128, 512], BF16) for i in range(2)]
            rs = [cx.sb("rs%d" % i, [128, 512], F32) for i in range(2)]
            qkst = cx.sb("qkst", [128, 8, 2, 512], BF16)
            tmst = [cx.sb("tmst%d" % i, [128, 4, 1024], BF16) for i in range(2)]
            it = 0
            for g in groups:
                T = g["T"]
                BLK = min(T, 512)
                for sidx in range(g["nseq"]):
                    for bi in range(T // BLK):
                        tb = bi * BLK
                        t0 = g["tok0"] + sidx * T + tb
                        nsub = BLK // 128
                        for c in range(24):
                            x_ = xin[it % 3]
                            lo = 2 if tb == 0 else 0
                            hi = 2 if tb + BLK == T else 0
                            if lo:
                                cx.memset(x_[:, 0:2], 0.0, eng="pool")
                            if hi:
                                cx.memset(x_[:, BLK + 2:BLK + 4], 0.0, eng="pool")
                            cx.dma("sp", x_[:, lo:BLK + 4 - hi], projT_d[c * 128:(c + 1) * 128, t0 - 2 + lo:t0 + BLK + 2 - hi])
                            ps = PS[it % 4]
                            it += 1
                            for k in range(5):
                                cx.mm(ps[:, 0:BLK], dg[:, c, k, :], x_[:, k:k + BLK], start=(k == 0), stop=(k == 4))
                            if c < 16:
                                cx.act(sil[:, c, 0:BLK], ps[:, 0:BLK], AF.Silu)
                            else:
                                cx.act(silv[:, c - 16, 0:BLK], ps[:, 0:BLK], AF.Silu)
                        for kind in range(3):
                            for h in range(8):
                                c = kind * 8 + h
                                if kind < 2:
                                    q_, r_ = sq[it % 2], rs[it % 2]
                                    cx.act(q_[:, 0:BLK], sil[:, c, 0:BLK], AF.Square)
                                    ps2 = PS[4 + it % 2]
                                    cx.mm(ps2[:, 0:BLK], ones_b[:], q_[:, 0:BLK])
                                    cx.act(r_[:, 0:BLK], ps2[:, 0:BLK], AF.Ln, bias=EPS)
                                    cx.act(r_[:, 0:BLK], r_[:, 0:BLK], AF.Exp, scale=-0.5)
                                    if kind == 0:
                                        cx.stt(qkst[:, h, 1, 0:BLK], sil[:, c, 0:BLK], 128.0 ** -0.5, r_[:, 0:BLK], ALU.mult, ALU.mult)
                                    else:
                                        cx.tt(qkst[:, h, 0, 0:BLK], sil[:, c, 0:BLK], r_[:, 0:BLK], ALU.mult)
                                if kind >= 1:
                                    src = silv[:, h, 0:BLK] if kind == 2 else qkst[:, h, 0, 0:BLK]
                                    tm = tmst[kind - 1]
                                    pb = PSB[it % 2]
                                    for s_i in range(nsub):
                                        cx.tr(pb[:, s_i, :], src[:, s_i * 128:(s_i + 1) * 128], ident_b[:])
                                    cx.copy(tm[:, 0:nsub, h * 128:(h + 1) * 128], pb[:, 0:nsub, :], eng=("act" if it % 2 else "dve"))
                                it += 1
                            if kind >= 1:
                                dst = ktm_d if kind == 1 else vtm_d
                                cx.dma("pool", dst[t0:t0 + BLK, :].rr("(s p) c -> p s c", p=128), tmst[kind - 1][:, 0:nsub, :])
                        cx.dma("pool", qk_d[:, :, :, t0:t0 + BLK].rr("h kq p t -> p h kq t"), qkst[:, :, :, 0:BLK])
        if stop_after == "p2a":
            continue
        cx.mark("L%d Phase 2b" % l)
        with cx.scope():
            NUMAX = 32
            f32r_ok = True
            ident_r = cx.sb("ident_r", [128, 128], F32R)
            cx.copy(ident_r[:], ident_f[:])
            selt = cx.sb("selt", [128, 2, 128], F32)
            cx.dma("sp", selt[:], sel_d[:])
            dtb = cx.sb("dtb", [128, 16], F32)
            nega = cx.sb("nega", [128, 16], F32)
            cx.dma("sp", dtb[:], dt_bias[l, :].pbc())
            cx.dma("sp", nega[:], a_log[l, :].pbc())
            cx.act(nega[:], nega[:], AF.Exp)
            cx.ts(nega[:], nega[:], -1.0, ALU.mult)
            qkT = cx.sb("qkT", [128, 8, 2, 512], BF16)
            ktm = cx.sb("ktm", [128, 4, 1024], BF16)
            vtm = cx.sb("vtm", [128, 4, 1024], BF16)
            abt = cx.sb("abt", [128, 4, 32], F32)
            gful = cx.sb("gful", [128, 4, 16], F32)
            bful = cx.sb("bful", [128, 4, 16], F32)
            gd = cx.sb("gd", [128, NUMAX], F32)
            gc = cx.sb("gc", [128, NUMAX], F32)
            tot = cx.sb("tot", [128, NUMAX], F32)
            egc = cx.sb("egc", [128, NUMAX], F32)
            ekg = cx.sb("ekg", [128, NUMAX], F32)
            nbeta = cx.sb("nbeta", [128, NUMAX], F32)
            gam = cx.sb("gam", [128, 2, NUMAX], F32)
            GS = 4
            ugS = [cx.sb("ugS%d" % i, [128, 4, 128], F32) for i in range(GS)]
            ztS = [cx.sb("ztS%d" % i, [128, 4, 128], F32) for i in range(GS)]
            esS = [cx.sb("esS%d" % i, [128, 4, 128], F32) for i in range(GS)]
            egS = [cx.sb("egS%d" % i, [128, 4, 128], BF16) for i in range(GS)]
            ZS = [cx.sb("ZS%d" % i, [128, 4, 128], F32) for i in range(GS)]
            R32 = [cx.sb("R32_%d" % i, [128, 4, 128], F32) for i in range(GS)]
            Zh = [cx.sb("Zh%d" % i, [128, 4, 128], BF16) for i in range(GS)]
            Zl = [cx.sb("Zl%d" % i, [128, 4, 128], BF16) for i in range(GS)]
            ZTh = [cx.sb("ZTh%d" % i, [128, 4, 128], BF16) for i in range(GS)]
            ZTl = [cx.sb("ZTl%d" % i, [128, 4, 128], BF16) for i in range(GS)]
            Rh = [cx.sb("Rh%d" % i, [128, 4, 128], BF16) for i in range(GS)]
            Rl = [cx.sb("Rl%d" % i, [128, 4, 128], BF16) for i in range(GS)]
            AttnT = cx.sb("AttnT", [128, NUMAX, 128], BF16)
            QgT = cx.sb("QgT", [128, NUMAX, 128], BF16)
            Rf = cx.sb("Rf", [128, NUMAX, 128], BF16)
            Kg = cx.sb("Kg", [128, NUMAX, 128], BF16)
            S32 = [cx.sb("S32_%d" % h, [128, 128], F32) for h in range(8)]
            Sbf = [cx.sb("Sbf_%d" % h, [128, 128], BF16) for h in range(8)]
            Xt = [cx.sb("Xt_%d" % h, [128, 128], BF16) for h in range(8)]
            Vn = [cx.sb("Vn_%d" % h, [128, 128], BF16) for h in range(8)]
            Ost = [cx.sb("Ost%d" % i, [128, 8, 512], BF16) for i in range(2)]
            for h in range(8):
                cx.memset(Xt[h][:], 0.0, eng="pool")
                cx.memset(Vn[h][:], 0.0, eng="pool")
            psKS = [PS[h].sub("psKS%d" % h) for h in range(8)]
            psAX = [PS[h].sub("psAX%d" % h) for h in range(8)]
            psO = [PS[h].sub("psO%d" % h) for h in range(8)]
            psDS = [PS[h].sub("psDS%d" % h) for h in range(8)]
            ident4 = ident_f[:].unsq(1).bcast([128, 4, 128])
            blki = 0
            for g in groups:
                T = g["T"]
                BLK = min(T, 512)
                nblk = T // BLK
                nt = BLK // 128
                NU = nt * 8
                for sidx in range(g["nseq"]):
                    for d in range(2):
                        od_d = of_d if d == 0 else ob_d
                        for h in range(8):
                            if g["ctx"]:
                                cx.dma("sp", S32[h][:], (s0f if d == 0 else s0b)[l, h, :, :])
                            else:
                                cx.memset(S32[h][:], 0.0, eng="pool")
                            cx.copy(Sbf[h][:], S32[h][:], eng="pool")
                        for bo in range(nblk):
                            bi = bo if d == 0 else nblk - 1 - bo
                            t0 = g["tok0"] + sidx * T + bi * BLK
                            cx.dma("sp", qkT[:, :, :, 0:BLK], qk_d[:, :, :, t0:t0 + BLK].rr("h kq p t -> p h kq t"))
                            cx.dma("sp", ktm[:, 0:nt, :], ktm_d[t0:t0 + BLK, :].rr("(s p) c -> p s c", p=128))
                            cx.dma("sp", vtm[:, 0:nt, :], vtm_d[t0:t0 + BLK, :].rr("(s p) c -> p s c", p=128))
                            cx.dma("sp", abt[:, 0:nt, :], ab_d[t0:t0 + BLK, :].rr("(s p) c -> p s c", p=128))
                            if cfg.get("p2b_stop", 99) <= 1:
                                continue
                            cx.mark("L%d 2b-ld" % l)
                            for s in range(nt):
                                cx.tt(gful[:, s, :], abt[:, s, 16:32], dtb[:], ALU.add)
                            cx.act(gful[:, 0:nt, :], gful[:, 0:nt, :], AF.Exp)
                            cx.act(gful[:, 0:nt, :], gful[:, 0:nt, :], AF.Ln, bias=1.0)
                            for s in range(nt):
                                cx.tt(gful[:, s, :], gful[:, s, :], nega[:], ALU.mult)
                            cx.act(bful[:, 0:nt, :], abt[:, 0:nt, 0:16], AF.Exp, scale=-1.0)
                            cx.ts(bful[:, 0:nt, :], bful[:, 0:nt, :], 1.0, ALU.add)
                            cx.recip(bful[:, 0:nt, :], bful[:, 0:nt, :])
                            if cfg.get("p2b_stop", 99) <= 2:
                                continue
                            gdv = gd[:, 0:NU].rr("p (s h) -> p s h", h=8)
                            cx.copy(gdv, gful[:, 0:nt, d * 8:(d + 1) * 8])
                            cx.ts(nbeta[:, 0:NU].rr("p (s h) -> p s h", h=8), bful[:, 0:nt, d * 8:(d + 1) * 8], -1.0, ALU.mult)
                            pz = PS[7]
                            cx.mm(pz[:, 0:NU], masks[:, d, :], gd[:, 0:NU])
                            cx.mm(pz[:, 32:32 + NU], masks[:, 6, :], gd[:, 0:NU])
                            cx.mm(pz[:, 64:64 + NU], selt[:, 0, :], gd[:, 0:NU])
                            cx.mm(pz[:, 96:96 + NU], selt[:, 1, :], gd[:, 0:NU])
                            cx.copy(gc[:, 0:NU], pz[:, 0:NU])
                            cx.act(egc[:, 0:NU], pz[:, 0:NU], AF.Exp)
                            cx.tt(tot[:, 0:NU], pz[:, 32:32 + NU], gc[:, 0:NU], ALU.subtract)
                            cx.act(ekg[:, 0:NU], tot[:, 0:NU], AF.Exp)
                            cx.act(gam[:, 0, 0:NU], pz[:, 64:64 + NU], AF.Exp)
                            cx.act(gam[:, 1, 0:NU], pz[:, 96:96 + NU], AF.Exp)
                            cx.mark("L%d 2b-pre" % l)
                            ngrp = NU // 4
                            for sg in range(0, ngrp, GS):
                                qs = list(range(min(GS, ngrp - sg)))
                                n0s = [(sg + q) * 4 for q in qs]
                                for q in qs:
                                    n0 = n0s[q]
                                    cx.tt(ugS[q][:], masks[:, d, :].unsq(1).bcast([128, 4, 128]), gd[:, n0:n0 + 4].unsq(2).bcast([128, 4, 128]), ALU.mult, eng="pool")
                                for q in qs:
                                    cx.mm(PS[q][:], ones_f[:], ugS[q][:].rr("p a b -> p (a b)"))
                                for q in qs:
                                    n0 = n0s[q]
                                    tt_, h0 = n0 // 8, n0 % 8
                                    pG3 = PS[q][:].rr("p (a b) -> p a b", a=4)
                                    cx.act(egS[q][:].rr("p a b -> p (a b)"), PS[q][:], AF.Exp)
                                    cx.tt(ztS[q][:], pG3, gc[:, n0:n0 + 4].unsq(2).bcast([128, 4, 128]), ALU.subtract)
                                    cx.tt(ztS[q][:], ztS[q][:], masks[:, 4 + d, :].unsq(1).bcast([128, 4, 128]), ALU.add, eng="pool")
                                    cx.tt(QgT[:, n0:n0 + 4, :], qkT[:, h0:h0 + 4, 1, tt_ * 128:(tt_ + 1) * 128], egS[q][:], ALU.mult, eng="pool")
                                    cx.tt(Kg[:, n0:n0 + 4, :], ktm[:, tt_, h0 * 128:(h0 + 4) * 128].rr("p (a b) -> p a b", a=4),
                                          ekg[:, n0:n0 + 4].unsq(2).bcast([128, 4, 128]), ALU.mult, eng="pool")
                                for q in qs:
                                    n0 = n0s[q]
                                    tt_, h0 = n0 // 8, n0 % 8
                                    cx.act(ztS[q][:].rr("p a b -> p (a b)"), ztS[q][:].rr("p a b -> p (a b)"), AF.Exp)
                                    cx.tt(esS[q][:], ztS[q][:], masks[:, 2 + d, :].unsq(1).bcast([128, 4, 128]), ALU.mult, eng="pool")
                                    cx.tt(esS[q][:], esS[q][:], nbeta[:, n0:n0 + 4].unsq(2).bcast([128, 4, 128]), ALU.mult, eng="pool")
                                    for j in range(4):
                                        h = h0 + j
                                        pP = PS[4 + q] if j < 2 else PS[q]
                                        cx.mm(pP[:, (j % 2) * 256:(j % 2 + 1) * 256].rr("p (a b) -> p a b", a=2), qkT[:, h, 0, tt_ * 128:(tt_ + 1) * 128],
                                              qkT[:, h, :, tt_ * 128:(tt_ + 1) * 128])
                                for q in qs:
                                    n0 = n0s[q]
                                    for jj in range(2):
                                        pP = PS[4 + q] if jj == 0 else PS[q]
                                        cx.tt(ZS[q][:, 2 * jj:2 * jj + 2, :], pP[:].rr("p (a b c) -> p a b c", a=2, b=2)[:, :, 0, :], esS[q][:, 2 * jj:2 * jj + 2, :], ALU.mult)
                                        cx.tt(AttnT[:, n0 + 2 * jj:n0 + 2 * jj + 2, :], pP[:].rr("p (a b c) -> p a b c", a=2, b=2)[:, :, 1, :], ztS[q][:, 2 * jj:2 * jj + 2, :], ALU.mult)
                                if cfg.get("skip_inv"):
                                    continue
                                def split(q, hi, lo, ps):
                                    cx.copy(hi[:].rr("p a b -> p (a b)"), ps[:], eng="act")
                                    cx.tt(lo[:].rr("p a b -> p (a b)"), ps[:], hi[:].rr("p a b -> p (a b)"), ALU.subtract)

                                def prod(ps, A, B, j):
                                    o = ps[:, j * 128:(j + 1) * 128]
                                    cx.mm(o, A[0][:, j, :], B[0][:, j, :], start=True, stop=False)
                                    cx.mm(o, A[0][:, j, :], B[1][:, j, :], start=False, stop=False)
                                    cx.mm(o, A[1][:, j, :], B[0][:, j, :], start=False, stop=True)
                                for q in qs:
                                    cx.copy(Zh[q][:], ZS[q][:], eng="act")
                                    cx.tt(Zl[q][:], ZS[q][:], Zh[q][:], ALU.subtract, eng="pool")
                                    cx.tt(R32[q][:], ZS[q][:], ident4, ALU.add)
                                for q in qs:
                                    for j in range(4):
                                        o = PS[q][:, j * 128:(j + 1) * 128]
                                        cx.mm(o, Zh[q][:, j, :], ident_b[:], start=True, stop=False)
                                        cx.mm(o, Zl[q][:, j, :], ident_b[:], start=False, stop=True)
                                for q in qs:
                                    split(q, ZTh[q], ZTl[q], PS[q])
                                    cx.copy(Rh[q][:], R32[q][:], eng="act")
                                    cx.tt(Rl[q][:], R32[q][:], Rh[q][:], ALU.subtract, eng="pool")
                                for kk in range(1, 6):
                                    for q in qs:
                                        for j in range(4):
                                            prod(PS[q], (Zh[q], Zl[q]), (ZTh[q], ZTl[q]), j)
                                        if kk < 5:
                                            for j in range(4):
                                                prod(PS[4 + q], (ZTh[q], ZTl[q]), (Zh[q], Zl[q]), j)
                                    for q in qs:
                                        split(q, ZTh[q], ZTl[q], PS[q])
                                        if kk < 5:
                                            split(q, Zh[q], Zl[q], PS[4 + q])
                                    for q in qs:
                                        for j in range(4):
                                            prod(PS[q], (ZTh[q], ZTl[q]), (Rh[q], Rl[q]), j)
                                    for q in qs:
                                        n0 = n0s[q]
                                        if kk < 5:
                                            cx.tt(R32[q][:].rr("p a b -> p (a b)"), PS[q][:], R32[q][:].rr("p a b -> p (a b)"), ALU.add)
                                            cx.copy(Rh[q][:], R32[q][:], eng="act")
                                            cx.tt(Rl[q][:], R32[q][:], Rh[q][:], ALU.subtract, eng="pool")
                                        else:
                                            cx.tt(Rf[:, n0:n0 + 4, :].rr("p a b -> p (a b)"), PS[q][:], R32[q][:].rr("p a b -> p (a b)"), ALU.add)
                            cx.mark("L%d 2b-scan" % l)
                            ost = Ost[blki % 2]
                            blki += 1
                            nch = BLK // 64
                            for co in range(0 if cfg.get("skip_scan") else nch):
                                c = co if d == 0 else nch - 1 - co
                                tt_, s = c // 2, c % 2
                                r0_, r1_ = s * 64, s * 64 + 64
                                cs = slice(c * 64, c * 64 + 64)
                                ns = [tt_ * 8 + h for h in range(8)]
                                for h in range(8):
                                    cx.mm(psKS[h][r0_:r1_, 0:128], qkT[:, h, 0, cs], Sbf[h][:])
                                for h in range(8):
                                    n = ns[h]
                                    cx.stt(Xt[h][r0_:r1_, :], psKS[h][r0_:r1_, 0:128], egc[r0_:r1_, n:n + 1], vtm[r0_:r1_, tt_, h * 128:(h + 1) * 128], ALU.mult, ALU.subtract)
                                for h in range(8):
                                    cx.mm(psAX[h][r0_:r1_, 128:256], Rf[:, ns[h], r0_:r1_], Xt[h][:])
                                for h in range(8):
                                    n = ns[h]
                                    cx.act(Vn[h][r0_:r1_, :], psAX[h][r0_:r1_, 128:256], AF.Copy, scale=nbeta[r0_:r1_, n:n + 1])
                                for h in range(8):
                                    n = ns[h]
                                    cx.mm(psO[h][:, 256:320], Sbf[h][:], QgT[:, n, r0_:r1_], start=True, stop=False)
                                    cx.mm(psO[h][:, 256:320], Vn[h][:], AttnT[:, n, r0_:r1_], start=False, stop=True)
                                    cx.mm(psDS[h][:, 320:448], Kg[r0_:r1_, n, :], Vn[h][r0_:r1_, :])
                                for h in range(8):
                                    n = ns[h]
                                    cx.stt(S32[h][:], S32[h][:], gam[:, s, n:n + 1], psDS[h][:, 320:448], ALU.mult, ALU.add)
                                for h in range(8):
                                    cx.copy(Sbf[h][:], S32[h][:], eng="pool")
                                    cx.copy(ost[:, h, cs], psO[h][:, 256:320], eng="act")
                            cx.dma("sp", od_d[:, t0:t0 + BLK].rr("(h p) t -> p h t", p=128), ost[:, :, 0:BLK])
                        if not g["ctx"]:
                            b = sidx
                            for h in range(8):
                                cx.dma("sp", (nsf if d == 0 else nsb)[b, l, h, :, :], S32[h][:])
        if stop_after == "p2b":
            continue
        cx.mark("L%d Phase 2c" % l)
        with cx.scope():
            onc_all = cx.sb("onc", [128, DEPTH], F32)
            cx.dma("sp", onc_all[:], onormT[:, :])
            onc = onc_all[:, l:l + 1]
            oft = [cx.sb("oft%d" % i, [128, 512], BF16) for i in range(2)]
            obt = [cx.sb("obt%d" % i, [128, 512], BF16) for i in range(2)]
            zat8 = cx.sb("zat8", [128, 8, 512], BF16)
            sz8 = cx.sb("sz8", [128, 8, 512], BF16)
            osum = [cx.sb("osum%d" % i, [128, 512], F32) for i in range(2)]
            osq = [cx.sb("osq%d" % i, [128, 512], BF16) for i in range(2)]
            rsd = [cx.sb("rsd%d" % i, [128, 512], F32) for i in range(2)]
            mst = [cx.sb("mst%d" % i, [128, 8, 512], BF16) for i in range(2)]
            it = 0
            for g in groups:
                ntt = g["nseq"] * g["T"] // 512
                for tt in range(ntt):
                    t0 = g["tok0"] + tt * 512
                    ms = mst[tt % 2]
                    for h in range(8):
                        cx.dma("sp", zat8[:, h, :], projT_d[(CH_ZA + h) * 128:(CH_ZA + h + 1) * 128, t0:t0 + 512])
                    for h in range(8):
                        cx.act(sz8[:, h, :], zat8[:, h, :], AF.Silu)
                    for h in range(8):
                        i2 = it % 2
                        it += 1
                        cx.dma("sp", oft[i2][:], of_d[h * 128:(h + 1) * 128, t0:t0 + 512])
                        cx.dma("sp", obt[i2][:], ob_d[h * 128:(h + 1) * 128, t0:t0 + 512])
                        cx.tt(osum[i2][:], oft[i2][:], obt[i2][:], ALU.add, eng="pool")
                        cx.act(osq[i2][:], osum[i2][:], AF.Square)
                        ps = PS[it % 4]
                        cx.mm(ps[:], ones_b[:], osq[i2][:])
                        cx.act(rsd[i2][:], ps[:], AF.Ln, scale=1.0 / 128, bias=EPS)
                        cx.act(rsd[i2][:], rsd[i2][:], AF.Exp, scale=-0.5)
                        cx.stt(osum[i2][:], osum[i2][:], onc, rsd[i2][:], ALU.mult, ALU.mult)
                        cx.tt(ms[:, h, :], osum[i2][:], sz8[:, h, :], ALU.mult)
                    cx.dma("pool", mixT_d[0:1024, t0:t0 + 512].rr("(h p) t -> p h t", p=128), ms[:])
        if stop_after == "p2c":
            continue
        cx.mark("L%d Phase 3" % l)
        with cx.scope():
            NKMAX = PAST + TS
            wuq = cx.sb("wuq", [128, 3, 1024], BF16)
            wukv = cx.sb("wukv", [128, 2, 1024], BF16)
            for c in range(3):
                cx.dma("pool", wuq[:, c, :], w_uq[l, c * 128:(c + 1) * 128, :])
            for c in range(2):
                cx.dma("pool", wukv[:, c, :], w_ukv[l, c * 128:(c + 1) * 128, :])
            qnc = cx.sb("qnc", [128, 3], F32)
            kvc = cx.sb("kvc", [128, 2], F32)
            cx.dma("sp", qnc[:], qnormT[:, l, :])
            cx.dma("sp", kvc[:], kvnormT[:, l, :])
            ckvnT = cx.sb("ckvnT", [128, 2, NKMAX], BF16)
            KT = cx.sb("KT", [128, 4, NKMAX], BF16)
            KPT = cx.sb("KPT", [64, NKMAX], BF16)
            Vtm = cx.sb("Vtm", [128, NKMAX // 128, 512], BF16)
            QT = [cx.sb("QT", [128, 4, 2, 512], BF16)] * 2
            rawT = [cx.sb("rawT", [128, 3, 512], BF16)] * 2
            sqT = [cx.sb("sqT", [128, 3, 512], BF16)] * 2
            rsm = [cx.sb("rsm", [128, 512], F32)] * 2
            cqn = [cx.sb("cqn", [128, 3, 512], BF16)] * 2
            ropet = [cx.sb("ropet", [64, 2, 512], F32)] * 2
            kpr = [cx.sb("kpr", [64, 2, 512], BF16)] * 2
            rt1 = [cx.sb("rt1", [64, 512], F32)] * 2
            rt2 = [cx.sb("rt2", [64, 512], F32)] * 2
            PT = [cx.sb("PT%d" % i, [128, 512], BF16) for i in range(3)]
            rden = [cx.sb("rden%d" % i, [128, 512], F32) for i in range(2)]
            zbt4 = cx.sb("zbt4", [128, 4, 512], BF16)
            accP = [cx.sb("accP%d" % i, [128, 512], F32) for i in range(2)]
            gzb4 = cx.sb("gzb4", [128, 4, 512], BF16)
            obst = [cx.sb("obst%d" % i, [128, 4, 512], BF16) for i in range(2)]
            cst = cx.sb("cst", [128, 2, 256], F32)
            cstb = cx.sb("cstb", [128, 2, 256], BF16)
            kst = cx.sb("kst", [128, 2, 64], F32)
            kstb = cx.sb("kstb", [128, 2, 64], BF16)
            PSB = [PS[6 + i][:].bitcast(BF16).rr("p (k t) -> p k t", k=8) for i in range(2)]
            SC = 192.0 ** -0.5
            it = 0
            for g in groups:
                T = g["T"]
                BLK = min(T, 512)
                nblk = T // BLK
                koff = PAST if g["ctx"] else 0
                NK = koff + T
                NKT = NK // 128
                for sidx in range(g["nseq"]):
                    ts0 = g["tok0"] + sidx * T
                    if g["ctx"]:
                        cx.dma("sp", cst[:], cckv[l, :, :].rr("(s p) c -> p s c", p=128))
                        cx.dma("sp", kst[:], ckpe[l, :, :].rr("(s p) c -> p s c", p=128))
                        cx.copy(cstb[:], cst[:])
                        cx.copy(kstb[:], kst[:])
                        pb = PSB[0]
                        for s in range(2):
                            for kc in range(2):
                                cx.tr(pb[:, s * 2 + kc, :], cstb[:, s, kc * 128:(kc + 1) * 128], ident_b[:])
                        for s in range(2):
                            for kc in range(2):
                                cx.copy(ckvnT[:, kc, s * 128:(s + 1) * 128], pb[:, s * 2 + kc, :])
                        pb = PSB[1]
                        for s in range(2):
                            cx.tr(pb[0:64, s, :], kstb[:, s, :], ident_b[:])
                        cx.copy(KPT[:, 0:256].rr("p (s t) -> p s t", s=2), pb[0:64, 0:2, :])
                    for bi in range(nblk):
                        t0 = ts0 + bi * BLK
                        k0 = koff + bi * BLK
                        i2 = it % 2
                        it += 1
                        raw, sq_, rs_ = rawT[i2], sqT[i2], rsm[i2]
                        cx.dma("sp", raw[:, 0:2, 0:BLK], projT_d[CH_CKV * 128:(CH_CKV + 2) * 128, t0:t0 + BLK].rr("(c p) t -> p c t", p=128))
                        cx.act(sq_[:, 0:2, 0:BLK], raw[:, 0:2, 0:BLK], AF.Square)
                        ps = PS[4 + i2]
                        for kc in range(2):
                            cx.mm(ps[:, 0:BLK], ones_b[:], sq_[:, kc, 0:BLK], start=(kc == 0), stop=(kc == 1))
                        cx.act(rs_[:, 0:BLK], ps[:, 0:BLK], AF.Ln, scale=1.0 / 256, bias=EPS)
                        cx.act(rs_[:, 0:BLK], rs_[:, 0:BLK], AF.Exp, scale=-0.5)
                        for kc in range(2):
                            cx.stt(ckvnT[:, kc, k0:k0 + BLK], raw[:, kc, 0:BLK], kvc[:, kc:kc + 1], rs_[:, 0:BLK], ALU.mult, ALU.mult)
                        if g["ctx"]:
                            kp_, ro_ = kpr[i2], ropet[i2]
                            cx.dma("sp", kp_[:, :, 0:BLK], projT_d[CH_KPE * 128:(CH_KPE + 1) * 128, t0:t0 + BLK].rr("(c p) t -> p c t", p=64))
                            cx.dma("sp", ro_[:, :, 0:BLK], rope_d[:, :, bi * BLK:(bi + 1) * BLK])
                            cx.tt(rt1[i2][:, 0:BLK], kp_[:, 0, 0:BLK], ro_[:, 0, 0:BLK], ALU.mult)
                            cx.tt(rt2[i2][:, 0:BLK], kp_[:, 1, 0:BLK], ro_[:, 1, 0:BLK], ALU.mult)
                            cx.tt(KPT[:, k0:k0 + BLK], rt1[i2][:, 0:BLK], rt2[i2][:, 0:BLK], ALU.add)
                        else:
                            cx.dma("sp", KPT[:, k0:k0 + BLK], projT_d[CH_KPE * 128:CH_KPE * 128 + 64, t0:t0 + BLK])
                    for kb in range(0, NK, 512):
                        n = min(512, NK - kb)
                        for h in range(4):
                            ps = PS[it % 4]
                            it += 1
                            for kc in range(2):
                                cx.mm(ps[:, 0:n], wukv[:, kc, h * 128:(h + 1) * 128], ckvnT[:, kc, kb:kb + n], start=(kc == 0), stop=(kc == 1))
                            cx.copy(KT[:, h, kb:kb + n], ps[:, 0:n], eng=("act" if it % 2 else "dve"))
                    for kt in range(NKT):
                        ps = PS[it % 4]
                        it += 1
                        for kc in range(2):
                            cx.mm(ps[:], ckvnT[:, kc, kt * 128:(kt + 1) * 128], wukv[:, kc, 512:1024], start=(kc == 0), stop=(kc == 1))
                        cx.copy(Vtm[:, kt, :], ps[:], eng=("act" if it % 2 else "dve"))
                    for bi in range(nblk):
                        t0 = ts0 + bi * BLK
                        i2 = it % 2
                        it += 1
                        raw, sq_, rs_, cq_, qt_ = rawT[i2], sqT[i2], rsm[i2], cqn[i2], QT[i2]
                        cx.dma("sp", raw[:, :, 0:BLK], projT_d[CH_CQ * 128:(CH_CQ + 3) * 128, t0:t0 + BLK].rr("(c p) t -> p c t", p=128))
                        cx.act(sq_[:, :, 0:BLK], raw[:, :, 0:BLK], AF.Square)
                        ps = PS[4 + i2]
                        for c in range(3):
                            cx.mm(ps[:, 0:BLK], ones_b[:], sq_[:, c, 0:BLK], start=(c == 0), stop=(c == 2))
                        cx.act(rs_[:, 0:BLK], ps[:, 0:BLK], AF.Ln, scale=1.0 / 384, bias=EPS)
                        cx.act(rs_[:, 0:BLK], rs_[:, 0:BLK], AF.Exp, scale=-0.5)
                        for c in range(3):
                            cx.stt(cq_[:, c, 0:BLK], raw[:, c, 0:BLK], qnc[:, c:c + 1], rs_[:, 0:BLK], ALU.mult, ALU.mult)
                        if g["ctx"]:
                            ro_ = ropet[i2]
                            cx.dma("sp", ro_[:, :, 0:BLK], rope_d[:, :, bi * BLK:(bi + 1) * BLK])
                        for h in range(4):
                            ps = PS[it % 4]
                            it += 1
                            for c in range(3):
                                cx.mm(ps[:, 0:BLK], wuq[:, c, h * 256:h * 256 + 128], cq_[:, c, 0:BLK], start=(c == 0), stop=(c == 2))
                            cx.copy(qt_[:, h, 0, 0:BLK], ps[:, 0:BLK], eng="act")
                            psa = PS[it % 4]
                            it += 1
                            for c in range(3):
                                cx.mm(psa[0:64, 0:BLK], wuq[:, c, h * 256 + 128:h * 256 + 192], cq_[:, c, 0:BLK], start=(c == 0), stop=(c == 2))
                            if g["ctx"]:
                                psb = PS[it % 4]
                                it += 1
                                for c in range(3):
                                    cx.mm(psb[0:64, 0:BLK], wuq[:, c, h * 256 + 192:h * 256 + 256], cq_[:, c, 0:BLK], start=(c == 0), stop=(c == 2))
                                cx.tt(rt1[i2][:, 0:BLK], psa[0:64, 0:BLK], ro_[:, 0, 0:BLK], ALU.mult)
                                cx.tt(rt2[i2][:, 0:BLK], psb[0:64, 0:BLK], ro_[:, 1, 0:BLK], ALU.mult)
                                cx.tt(qt_[0:64, h, 1, 0:BLK], rt1[i2][:, 0:BLK], rt2[i2][:, 0:BLK], ALU.add)
                            else:
                                cx.copy(qt_[0:64, h, 1, 0:BLK], psa[0:64, 0:BLK], eng="dve")
                        ob_ = obst[i2]
                        for h in range(4):
                            cx.dma("sp", zbt4[:, h, 0:BLK], projT_d[(CH_ZB + h) * 128:(CH_ZB + h + 1) * 128, t0:t0 + BLK])
                        for h in range(4):
                            cx.act(gzb4[:, h, 0:BLK], zbt4[:, h, 0:BLK], AF.Silu)
                        seqs = [(h, kt) for h in range(4) for kt in range(NKT)]

                        def emit_s(idx):
                            h, kt = seqs[idx]
                            pss = PS[idx % 4]
                            cx.mm(pss[:, 0:BLK], KT[:, h, kt * 128:(kt + 1) * 128], qt_[:, h, 0, 0:BLK], start=True, stop=False)
                            cx.mm(pss[:, 0:BLK], KPT[:, kt * 128:(kt + 1) * 128], qt_[0:64, h, 1, 0:BLK], start=False, stop=True)
                        emit_s(0)
                        for idx, (h, kt) in enumerate(seqs):
                            pso, psd = PS[4 + h % 2], PS[6 + h % 2]
                            if idx + 1 < len(seqs):
                                emit_s(idx + 1)
                            pt = PT[idx % 3]
                            cx.act(pt[:, 0:BLK], PS[idx % 4][:, 0:BLK], AF.Exp, scale=SC)
                            cx.mm(pso[:, 0:BLK], Vtm[:, kt, h * 128:(h + 1) * 128], pt[:, 0:BLK], start=(kt == 0), stop=(kt == NKT - 1))
                            ap_ = accP[h % 2]
                            if kt == 0:
                                cx.copy(ap_[:, 0:BLK], pt[:, 0:BLK], eng="dve")
                            else:
                                cx.tt(ap_[:, 0:BLK], ap_[:, 0:BLK], pt[:, 0:BLK], ALU.add, eng="dve")
                            if kt == NKT - 1:
                                j2 = h % 2
                                cx.mm(psd[:, 0:BLK], ones_f[:], ap_[:, 0:BLK])
                                cx.act(rden[j2][:, 0:BLK], psd[:, 0:BLK], AF.Ln)
                                cx.act(rden[j2][:, 0:BLK], rden[j2][:, 0:BLK], AF.Exp, scale=-1.0)
                                cx.tt(rden[j2][:, 0:BLK], rden[j2][:, 0:BLK], gzb4[:, h, 0:BLK], ALU.mult)
                                cx.tt(ob_[:, h, 0:BLK], pso[:, 0:BLK], rden[j2][:, 0:BLK], ALU.mult)
                        cx.dma("pool", mixT_d[1024:1536, t0:t0 + BLK].rr("(h p) t -> p h t", p=128), ob_[:, :, 0:BLK])
        if stop_after == "p3":
            continue
        cx.mark("L%d Phase 4" % l)
        with cx.scope():
            wpl = cx.sb("wpl", [128, 4, 128], BF16)
            for gi in range(4):
                cx.dma("pool", wpl[:, gi, :], w_pool[l, gi, :, :])
            psc = cx.sb("psc", [128, 4], F32)
            cx.dma("sp", psc[:], pscaleT[:, l, :])
            corr = cx.sb("corr", [128, 4, 16], F32)
            cx.dma("sp", corr[:].rr("p a b -> p (a b)"), corr_d[:, :].rr("a b -> (a b)").pbc())
            W = 528
            xct = [cx.sb("xct%d" % i, [128, W], BF16) for i in range(2)]
            xf = [cx.sb("xf%d" % i, [128, W], F32) for i in range(2)]
            aa = cx.sb("aa", [128, W], F32)
            bb_ = cx.sb("bb_", [128, W], F32)
            cc = cx.sb("cc", [128, W], F32)
            sm = cx.sb("sm", [128, 512], F32)
            pooled = [cx.sb("pooled%d" % i, [128, 512], BF16) for i in range(2)]
            zct = [cx.sb("zct%d" % i, [128, 512], BF16) for i in range(2)]
            gzc = [cx.sb("gzc%d" % i, [128, 512], F32) for i in range(2)]
            ocst = [cx.sb("ocst%d" % i, [128, 4, 512], BF16) for i in range(2)]
            it = 0
            for g in groups:
                T = g["T"]
                BLK = min(T, 512)
                nblk = T // BLK
                for sidx in range(g["nseq"]):
                    for bi in range(nblk):
                        t0 = g["tok0"] + sidx * T + bi * BLK
                        first, lastb = (bi == 0), (bi == nblk - 1)
                        oc_ = ocst[(it // 4) % 2]
                        for gi, win in enumerate((2, 4, 8, 16)):
                            i2 = it % 2
                            it += 1
                            x_, f_ = xct[i2], xf[i2]
                            lo = 8 if first else 0
                            hi = 8 if lastb else 0
                            if lo:
                                cx.memset(x_[:, 0:8], 0.0, eng="pool")
                            if hi:
                                cx.memset(x_[:, BLK + 8:BLK + 16], 0.0, eng="pool")
                            ch = CH_XC + gi
                            cx.dma("sp", x_[:, lo:BLK + 16 - hi], projT_d[ch * 128:(ch + 1) * 128, t0 - 8 + lo:t0 + BLK + 8 - hi])
                            cx.copy(f_[:, 0:BLK + 16], x_[:, 0:BLK + 16], eng="pool")
                            WW = BLK + 16
                            cx.tt(aa[:, 1:WW], f_[:, 0:WW - 1], f_[:, 1:WW], ALU.add)
                            if win == 2:
                                src = aa[:, 8:8 + BLK]
                            elif win == 4:
                                cx.tt(sm[:, 0:BLK], aa[:, 7:7 + BLK], aa[:, 9:9 + BLK], ALU.add)
                                src = sm[:, 0:BLK]
                            else:
                                cx.tt(bb_[:, 3:WW], aa[:, 1:WW - 2], aa[:, 3:WW], ALU.add)
                                if win == 8:
                                    cx.tt(sm[:, 0:BLK], bb_[:, 7:7 + BLK], bb_[:, 11:11 + BLK], ALU.add)
                                else:
                                    cx.tt(cc[:, 7:WW], bb_[:, 3:WW - 4], bb_[:, 7:WW], ALU.add)
                                    cx.tt(sm[:, 0:BLK], cc[:, 7:7 + BLK], cc[:, 15:15 + BLK], ALU.add)
                                src = sm[:, 0:BLK]
                            if first or lastb:
                                if win == 2:
                                    cx.copy(sm[:, 0:BLK], src)
                                    src = sm[:, 0:BLK]
                                if first:
                                    cx.tt(sm[:, 0:8], sm[:, 0:8], corr[:, gi, 0:8], ALU.mult)
                                if lastb:
                                    cx.tt(sm[:, BLK - 8:BLK], sm[:, BLK - 8:BLK], corr[:, gi, 8:16], ALU.mult)
                            pl = pooled[i2]
                            cx.stt(pl[:, 0:BLK], src, 1.0 / win, f_[:, 8:8 + BLK], ALU.mult, ALU.subtract)
                            ps = PS[it % 4]
                            cx.mm(ps[:, 0:BLK], wpl[:, gi, :], pl[:, 0:BLK])
                            cx.dma("sp", zct[i2][:, 0:BLK], projT_d[(CH_ZC + gi) * 128:(CH_ZC + gi + 1) * 128, t0:t0 + BLK])
                            cx.act(gzc[i2][:, 0:BLK], zct[i2][:, 0:BLK], AF.Silu)
                            cx.stt(oc_[:, gi, 0:BLK], ps[:, 0:BLK], psc[:, gi:gi + 1], gzc[i2][:, 0:BLK], ALU.mult, ALU.mult)
                        cx.dma("pool", mixT_d[1536:2048, t0:t0 + BLK].rr("(h p) t -> p h t", p=128), oc_[:, :, 0:BLK])
        if stop_after == "p4":
            continue
        cx.mark("L%d Phase 5" % l)
        with cx.scope():
            wo = cx.sb("wo", [128, 16, D], BF16)
            for k in range(16):
                cx.dma("pool", wo[:, k, :], w_out[l, k * 128:(k + 1) * 128, :])
            mt = [cx.sb("mt%d" % i, [128, 16, 128], BF16) for i in range(2)]
            xr = [cx.sb("xr%d" % i, [128, D], F32) for i in range(2)]
            yo = [cx.sb("yo%d" % i, [128, D], F32) for i in range(2)]
            junk5 = cx.sb("junk5", [128, 512], BF16)
            s5 = cx.sb("s5", [128, 8], F32)
            for g in groups:
                xin = (xp if g["name"] == "p" else xs) if l == 0 else (yp if g["name"] == "p" else ys)
                xout = yp if g["name"] == "p" else ys
                ntile = g["nseq"] * g["T"] // 128
                for ti in range(ntile):
                    i2 = ti % 2
                    tk = g["tok0"] + ti * 128
                    cx.dma("sp", mt[i2][:], mixT_d[:, tk:tk + 128].rr("(k p) t -> p k t", p=128))
                    cx.dma("sp", xr[i2][:], xin[ti * 128:(ti + 1) * 128, :])
                    for nb_ in range(4):
                        ps = PS[(ti % 2) * 4 + nb_]
                        for k in range(16):
                            cx.mm(ps[:], mt[i2][:, k, :], wo[:, k, nb_ * 512:(nb_ + 1) * 512], start=(k == 0), stop=(k == 15))
                        cx.act(junk5[:], ps[:], AF.Square, accum=s5[:, nb_:nb_ + 1])
                    cx.reduce(s5[:, 4:5], s5[:, 0:4], ALU.add)
                    cx.act(s5[:, 5:6], s5[:, 4:5], AF.Ln, scale=1.0 / D, bias=EPS)
                    cx.act(s5[:, 6:7], s5[:, 5:6], AF.Exp, scale=-0.5)
                    for nb_ in range(4):
                        ps = PS[(ti % 2) * 4 + nb_]
                        cs = slice(nb_ * 512, (nb_ + 1) * 512)
                        cx.stt(yo[i2][:, cs], ps[:], s5[:, 6:7], modbc[:, g["r"], 2, cs], ALU.mult, ALU.mult)
                    cx.tt(yo[i2][:], yo[i2][:], xr[i2][:], ALU.add, eng="pool")
                    cx.dma("pool", xout[ti * 128:(ti + 1) * 128, :], yo[i2][:])


_NC_CACHE = {}


def kernel(**inputs):
    from concourse.bass_utils import run_bass_kernel_spmd
    inp = {k: np.asarray(v) for k, v in inputs.items()}
    shared, per = host_prep(inp)
    if "nc" not in _NC_CACHE:
        _NC_CACHE["nc"] = build({})
    nc = _NC_CACHE["nc"]
    in_maps = [dict(shared, **per[c]) for c in range(NCORE)]
    res = run_bass_kernel_spmd(nc, in_maps, core_ids=list(range(NCORE)))
    rs = res.results
    y_prompt = np.concatenate([r["yp"].reshape(NP, TP, D) for r in rs], axis=0).astype(np.float32)
    y_sample = np.stack([r["ys"] for r in rs], axis=0).astype(np.float32)
    new_ckv = np.concatenate([r["nckv"] for r in rs], axis=0).astype(np.float32)
    new_kpe = np.concatenate([r["nkpe"] for r in rs], axis=0).astype(np.float32)
    new_sf = np.concatenate([r["nsf"] for r in rs], axis=0).astype(np.float32)
    new_sb = np.concatenate([r["nsb"] for r in rs], axis=0).astype(np.float32)
    return (y_prompt, y_sample, new_ckv, new_kpe, new_sf, new_sb)
```

```python
import numpy as np
import concourse.bass as bass
import concourse.mybir as mybir
from concourse.alu_op_type import AluOpType as ALU
from contextlib import ExitStack, contextmanager

F32 = mybir.dt.float32
BF16 = mybir.dt.bfloat16
F32R = mybir.dt.float32r
AF = mybir.ActivationFunctionType
AX = mybir.AxisListType


class TV:
    __slots__ = ("obj", "ap")

    def __init__(self, obj, ap):
        self.obj = obj
        self.ap = ap

    def __getitem__(self, idx):
        return TV(self.obj, self.ap[idx])

    def bitcast(self, dt):
        return TV(self.obj, self.ap.bitcast(dt))

    def bcast(self, shape):
        return TV(self.obj, self.ap.broadcast_to(shape))

    def pbc(self, n=128):
        return TV(self.obj, self.ap.partition_broadcast(n))

    def unsq(self, ax):
        return TV(self.obj, self.ap.unsqueeze(ax))

    def rr(self, pat, **kw):
        return TV(self.obj, self.ap.rearrange(pat, **kw))


class Trk:
    def __init__(self, name, kind):
        self.name = name
        self.kind = kind
        self.lw = None
        self.rd = {}
        self.sem = None
        self.dma_last = 0
        self.dma_kind = None
        self.dw = {}
        self.dr = {}


class Tile(Trk):
    def __init__(self, name, kind, handle):
        super().__init__(name, kind)
        self.h = handle
        self.bank = self
        self.acc = {}

    def __getitem__(self, idx):
        return TV(self, self.h[idx])

    def sub(self, name):
        t = Tile(name, self.kind, self.h)
        t.bank = self.bank
        return t


class Dram(Trk):
    def __init__(self, name, ap):
        super().__init__(name, "dram")
        self.apx = ap

    def __getitem__(self, idx):
        return TV(self, self.apx[idx])

    def v(self, ap):
        return TV(self, ap)


class DSem:
    def __init__(self, obj):
        self.obj = obj
        self.count = 0


class Ctx:
    NDSEM = 96

    def __init__(self, nc, es):
        self.nc = nc
        self.es = es
        self.eng = {"pe": nc.tensor, "act": nc.scalar, "dve": nc.vector, "pool": nc.gpsimd, "sp": nc.sync}
        self.sem = {}
        self.seq = {}
        for k in self.eng:
            self.sem[k] = es.enter_context(nc.semaphore("s_" + k))
            self.seq[k] = 0
        self.waited = {k: {} for k in self.eng}
        self.free_dsems = [DSem(es.enter_context(nc.semaphore("d%d" % i))) for i in range(self.NDSEM)]
        self.dma_tiles = []
        self.scope_tiles = [[]]
        self.n_instr = 0
        self.n_wait = 0
        self.drams = []

    def mark(self, label):
        if not hasattr(self, "marks"):
            self.marks = []
        self.marks.append((label, dict(self.seq)))

    def sb(self, name, shape, dt):
        self.uid = getattr(self, "uid", 0) + 1
        name = "sb%d_%s" % (self.uid, name)
        h = self.es.enter_context(self.nc.sbuf_tensor(name, list(shape), dt))
        return Tile(name, "sb", h)

    def ps(self, name, shape, dt):
        h = self.es.enter_context(self.nc.psum_tensor("pp_" + name, list(shape), dt))
        return Tile(name, "ps", h)

    def dram(self, name, shape, dt, kind=None):
        if kind is None:
            t = self.nc.dram_tensor(name, list(shape), dt)
        else:
            t = self.nc.dram_tensor(name, list(shape), dt, kind=kind)
        d = Dram(name, t.ap())
        self.drams.append(d)
        return d

    def _wait(self, e, semkey, semobj, val):
        w = self.waited[e]
        if w.get(semkey, 0) >= val:
            return
        self.eng[e].wait_ge(semobj, val)
        w[semkey] = val
        self.n_wait += 1

    def _wait_eng(self, e, p, seq):
        if p == "pe" and e == "pe":
            return
        self._wait(e, p, self.sem[p], seq)

    def _sync_dma(self, e, t):
        if t.sem is not None and t.dma_last > 0:
            self._wait(e, id(t.sem), t.sem.obj, 16 * t.dma_last)

    def _deps(self, e, reads, writes, same_engine_war=False):
        for t in reads:
            if t.lw is not None:
                self._wait_eng(e, *t.lw)
            self._sync_dma(e, t)
        for t in writes:
            if t.lw is not None and t.lw[0] != e:
                self._wait_eng(e, *t.lw)
            for p, s in t.rd.items():
                if p != e:
                    self._wait_eng(e, p, s)
            self._sync_dma(e, t)

    def op(self, e, fn, reads, writes):
        reads = [r.obj if isinstance(r, TV) else r for r in reads]
        writes = [w.obj if isinstance(w, TV) else w for w in writes]
        self._deps(e, reads, writes)
        banks = []
        for t in reads + writes:
            if t.kind == "ps" and t.bank not in banks:
                banks.append(t.bank)
                for p, sq in t.bank.acc.items():
                    if p != e:
                        self._wait_eng(e, p, sq)
        ins = fn()
        self.seq[e] += 1
        ins.then_inc(self.sem[e], 1)
        s = self.seq[e]
        for b in banks:
            b.acc[e] = s
        for t in reads:
            if t.rd.get(e, 0) < s:
                t.rd[e] = s
        for t in writes:
            t.lw = (e, s)
            t.rd = {}
        self.n_instr += 1
        return ins

    def _tsem(self, t):
        if t.sem is None:
            t.sem = self.free_dsems.pop()
            self.dma_tiles.append(t)
            self.scope_tiles[-1].append(t)
        return t.sem

    @contextmanager
    def scope(self):
        old = self.es
        self.scope_tiles.append([])
        with ExitStack() as es2:
            self.es = es2
            try:
                yield
            finally:
                self.barrier()
                self.es = old
                for t in self.scope_tiles.pop():
                    self.free_dsems.append(t.sem)
                    self.dma_tiles.remove(t)
                    t.sem = None

    def dma(self, q, out, in_, **kw):
        e = q
        src, dst = in_.obj, out.obj
        if dst.kind == "sb":
            t = dst
            kind = "load"
        elif src.kind == "sb":
            t = src
            kind = "store"
        else:
            raise ValueError("dram->dram dma unsupported here")
        sem = self._tsem(t)
        if src.kind == "sb":
            if src.lw is not None:
                self._wait_eng(e, *src.lw)
            if src is not t or (t.dma_kind is not None and t.dma_kind != kind):
                self._sync_dma(e, src)
        if dst.kind == "sb":
            if dst.lw is not None:
                self._wait_eng(e, *dst.lw)
            for p, s in dst.rd.items():
                self._wait_eng(e, p, s)
            if dst is not t or (t.dma_kind is not None and t.dma_kind != kind):
                self._sync_dma(e, dst)
        if src.kind == "dram":
            for k, (so, cnt) in src.dw.items():
                self._wait(e, k, so, 16 * cnt)
        if dst.kind == "dram":
            for k, (so, cnt) in dst.dr.items():
                self._wait(e, k, so, 16 * cnt)
        ins = self.eng[e].dma_start(out=out.ap, in_=in_.ap, **kw)
        ins.then_inc(sem.obj, 16)
        sem.count += 1
        t.dma_last = sem.count
        t.dma_kind = kind
        if src.kind == "dram":
            src.dr[id(sem)] = (sem.obj, sem.count)
        if dst.kind == "dram":
            dst.dw[id(sem)] = (sem.obj, sem.count)
        if src.kind == "sb" and dst.kind == "sb":
            pass
        self.n_instr += 1
        return ins

    def barrier(self):
        for t in self.dma_tiles:
            self._sync_dma("sp", t)
        names = list(self.eng)
        for e in names:
            for p in names:
                if p != e and self.seq[p] > 0 and not (p == "sp"):
                    self._wait(e, p, self.sem[p], self.seq[p])
        ins = self.nc.sync.nop()
        self.seq["sp"] += 1
        ins.then_inc(self.sem["sp"], 1)
        for e in names:
            if e != "sp":
                self._wait(e, "sp", self.sem["sp"], self.seq["sp"])
        for d in self.drams:
            d.dw = {}
            d.dr = {}

    def final_wait(self):
        for t in self.dma_tiles:
            self._sync_dma("sp", t)

    def mm(self, out, lhsT, rhs, start=True, stop=True):
        return self.op("pe", lambda: self.nc.tensor.matmul(out.ap, lhsT=lhsT.ap, rhs=rhs.ap, start=start, stop=stop),
                       [lhsT, rhs] + ([] if start else [out]), [out])

    def tr(self, out, in_, ident):
        return self.op("pe", lambda: self.nc.tensor.transpose(out.ap, in_.ap, ident.ap), [in_, ident], [out])

    def act(self, out, in_, func, bias=None, scale=None, accum=None, eng="act"):
        kw = {}
        rd = [in_]
        wr = [out]
        if bias is not None:
            if isinstance(bias, TV):
                kw["bias"] = bias.ap
                rd.append(bias)
            else:
                kw["bias"] = bias
        if scale is not None:
            if isinstance(scale, TV):
                kw["scale"] = scale.ap
                rd.append(scale)
            else:
                kw["scale"] = scale
        if accum is not None:
            kw["accum_out"] = accum.ap
            wr.append(accum)
        return self.op("act", lambda: self.nc.scalar.activation(out=out.ap, in_=in_.ap, func=func, **kw), rd, wr)

    def _e(self, eng):
        return self.eng[eng]

    def tt(self, out, a, b, op, eng="dve"):
        return self.op(eng, lambda: self._e(eng).tensor_tensor(out=out.ap, in0=a.ap, in1=b.ap, op=op), [a, b], [out])

    def ts(self, out, a, s1, op0, s2=None, op1=None, eng="dve", accum=None):
        rd = [a]
        wr = [out]
        v1 = s1.ap if isinstance(s1, TV) else s1
        v2 = s2.ap if isinstance(s2, TV) else s2
        if isinstance(s1, TV):
            rd.append(s1)
        if isinstance(s2, TV):
            rd.append(s2)
        kw = {}
        if op1 is not None:
            kw["op1"] = op1
        if accum is not None:
            kw["accum_out"] = accum.ap
            wr.append(accum)
        return self.op(eng, lambda: self._e(eng).tensor_scalar(out=out.ap, in0=a.ap, scalar1=v1, scalar2=v2, op0=op0, **kw), rd, wr)

    def stt(self, out, a, s, b, op0, op1, eng="dve"):
        rd = [a, b]
        v = s.ap if isinstance(s, TV) else s
        if isinstance(s, TV):
            rd.append(s)
        return self.op(eng, lambda: self._e(eng).scalar_tensor_tensor(out=out.ap, in0=a.ap, scalar=v, in1=b.ap, op0=op0, op1=op1), rd, [out])

    def copy(self, out, in_, eng="dve"):
        if eng == "act":
            return self.op("act", lambda: self.nc.scalar.copy(out=out.ap, in_=in_.ap), [in_], [out])
        return self.op(eng, lambda: self._e(eng).tensor_copy(out=out.ap, in_=in_.ap), [in_], [out])

    def recip(self, out, in_):
        return self.op("dve", lambda: self.nc.vector.reciprocal(out=out.ap, in_=in_.ap), [in_], [out])

    def memset(self, out, val, eng="dve"):
        return self.op(eng, lambda: self._e(eng).memset(out.ap, val), [], [out])

    def reduce(self, out, in_, op, axis=AX.X):
        return self.op("dve", lambda: self.nc.vector.tensor_reduce(out=out.ap, in_=in_.ap, axis=axis, op=op), [in_], [out])


D = 2048
DEPTH = 4
NCORE = 8
TP = 256
NP = 4
TS = 4096
PAST = 256
NTOK = NP * TP + TS
H_A = 8
EPS = 1e-6
NFM = 50
NTM = 352
CH_Q, CH_K, CH_V, CH_ZA, CH_CQ, CH_CKV, CH_ZB, CH_XC, CH_ZC, CH_KPE = 0, 8, 16, 24, 32, 35, 37, 41, 45, 49
PASSES = [(0, 13), (13, 26), (26, 38), (38, 50)]


def _swap_perm():
    p = np.arange(64).reshape(2, 2, 16)
    return p[:, ::-1, :].reshape(64)


def host_prep(inp):
    f32 = np.float32
    w_in = inp["w_in"]
    o = np.cumsum([0, 1024, 1024, 1024, 1024, 16, 16, 384, 256, 64, 512, 512, 512])
    (oq, ok, ov, oz, ob, oa, ocq, ockv, okpe, ozb, oxc, ozc, oend) = o
    sw = _swap_perm()
    cols = np.concatenate([
        np.arange(oq, oq + 4096),
        np.arange(ocq, ocq + 384),
        np.arange(ockv, ockv + 256),
        np.arange(ozb, ozb + 512),
        np.arange(oxc, oxc + 512),
        np.arange(ozc, ozc + 512),
        np.arange(okpe, okpe + 64), okpe + sw,
    ])
    assert cols.size == NFM * 128
    w_in_fm = np.ascontiguousarray(w_in[:, :, cols])
    cols_tm = np.concatenate([np.arange(ob, ob + 32), np.arange(ockv, ockv + 256), np.arange(okpe, okpe + 64)])
    w_in_tm = np.ascontiguousarray(w_in[:, :, cols_tm])
    wq = inp["w_uq"].reshape(DEPTH, 384, 4, 192)
    w_uq_ext = np.ascontiguousarray(np.concatenate([wq[..., :128], wq[..., 128:], wq[..., 128:][..., sw]], axis=-1).reshape(DEPTH, 384, 1024))
    wkv = inp["w_ukv"].reshape(DEPTH, 256, 4, 256)
    w_ukv_ext = np.ascontiguousarray(np.concatenate([wkv[..., :128].reshape(DEPTH, 256, 512), wkv[..., 128:].reshape(DEPTH, 256, 512)], axis=-1))
    convT = np.ascontiguousarray(inp["conv_w"].reshape(DEPTH, 5, 24, 128).transpose(3, 0, 2, 1))
    colp = lambda a, n: np.ascontiguousarray(a.reshape(DEPTH, n, 128).transpose(2, 0, 1))
    idx = np.arange(128)
    same = (idx[:, None] // 64) == (idx[None, :] // 64)
    vf = same & (idx[None, :] >= idx[:, None])
    vb = same & (idx[None, :] <= idx[:, None])
    masks = np.zeros((128, 8, 128), f32)
    masks[:, 0] = vf
    masks[:, 1] = vb
    masks[:, 2] = same & (idx[None, :] > idx[:, None])
    masks[:, 3] = same & (idx[None, :] < idx[:, None])
    masks[:, 4] = (vf.astype(f32) - 1.0) * 30000.0
    masks[:, 5] = (vb.astype(f32) - 1.0) * 30000.0
    masks[:, 6] = same
    masks[:, 7] = np.eye(128)
    sel = np.zeros((128, 2, 128), f32)
    sel[:64, 0] = 1
    sel[64:, 1] = 1
    T = TS
    rows = T // 64
    row = np.repeat(np.arange(rows, dtype=f32), 64)
    col = np.tile(np.arange(64, dtype=f32), rows)
    nf = 16
    inv = (10000.0 ** (-np.arange(nf, dtype=f32) / nf)).astype(f32)
    ang = np.stack([row[:, None] * inv, col[:, None] * inv], axis=1)
    cos, sin = np.cos(ang).astype(f32), np.sin(ang).astype(f32)
    COS = np.stack([cos, cos], axis=2).reshape(T, 64).T
    SIN = np.stack([-sin, sin], axis=2).reshape(T, 64).T
    rope = np.ascontiguousarray(np.stack([COS, SIN], axis=1)).astype(f32)
    corr = np.ones((4, 16), f32)
    for gi, win in enumerate((2, 4, 8, 16)):
        lo = win // 2
        hi = win - 1 - lo
        for t in range(8):
            corr[gi, t] = win / (min(t, lo) + hi + 1)
            corr[gi, 8 + t] = win / (lo + 1 + min(hi, 7 - t))
    shared = {
        "w_ada": inp["w_ada"], "b_ada": inp["b_ada"], "norm_pre": inp["norm_pre"], "norm_post": inp["norm_post"],
        "w_in_fm": w_in_fm, "w_in_tm": w_in_tm, "convT": convT,
        "a_log": np.ascontiguousarray(inp["a_log"].reshape(DEPTH, 16)), "dt_bias": np.ascontiguousarray(inp["dt_bias"].reshape(DEPTH, 16)),
        "onormT": np.ascontiguousarray(inp["o_norm_a"].T), "qnormT": colp(inp["q_norm"], 3), "kvnormT": colp(inp["kv_norm"], 2),
        "kv_norm": inp["kv_norm"], "pscaleT": colp(inp["pool_scale"], 4),
        "w_uq": w_uq_ext, "w_ukv": w_ukv_ext, "w_pool": inp["w_pool"], "w_out": inp["w_out"],
        "masks": masks, "sel": sel, "rope": rope, "corr": corr,
    }
    per = []
    for c in range(NCORE):
        c2 = np.stack([inp["c_ctx"], inp["c"][c]], axis=0)
        cT = np.ascontiguousarray(c2.reshape(2, 16, 128).transpose(2, 1, 0))
        per.append({
            "xp": np.ascontiguousarray(inp["x_prompt"][c * NP:(c + 1) * NP].reshape(NP * TP, D)),
            "xs": np.ascontiguousarray(inp["x_sample"][c]),
            "cckv": np.ascontiguousarray(inp["cache_ckv"][c]), "ckpe": np.ascontiguousarray(inp["cache_kpe"][c]),
            "s0f": np.ascontiguousarray(inp["state_delta_fwd"][c]), "s0b": np.ascontiguousarray(inp["state_delta_bwd"][c]),
            "cT": cT,
        })
    return shared, per


def build(cfg):
    depth = cfg.get("depth", DEPTH)
    stop_after = cfg.get("stop_after", None)
    nc = bass.Bass("TRN2", target_bir_lowering=False)
    es = ExitStack()
    with es:
        cx = Ctx(nc, es)
        _program(cx, cfg, depth, stop_after)
        cx.final_wait()
        print("instr", cx.n_instr, "waits", cx.n_wait, "seq", cx.seq)
        build.marks = getattr(cx, "marks", [])
    return nc


def _program(cx, cfg, depth, stop_after):
    nc = cx.nc
    IN = lambda n, s, dt=F32: cx.dram(n, s, dt, kind="ExternalInput")
    OUT = lambda n, s, dt=F32: cx.dram(n, s, dt, kind="ExternalOutput")
    xp = IN("xp", [NP * TP, D]); xs = IN("xs", [TS, D])
    cckv = IN("cckv", [DEPTH, PAST, 256]); ckpe = IN("ckpe", [DEPTH, PAST, 64])
    s0f = IN("s0f", [DEPTH, 8, 128, 128]); s0b = IN("s0b", [DEPTH, 8, 128, 128])
    cT = IN("cT", [128, 16, 2])
    w_ada = IN("w_ada", [DEPTH, D, 3 * D]); b_ada = IN("b_ada", [DEPTH, 3 * D])
    norm_pre = IN("norm_pre", [DEPTH, D]); norm_post = IN("norm_post", [DEPTH, D])
    w_in_fm = IN("w_in_fm", [DEPTH, D, NFM * 128]); w_in_tm = IN("w_in_tm", [DEPTH, D, NTM])
    convT = IN("convT", [128, DEPTH, 24, 5])
    a_log = IN("a_log", [DEPTH, 16]); dt_bias = IN("dt_bias", [DEPTH, 16])
    onormT = IN("onormT", [128, DEPTH]); qnormT = IN("qnormT", [128, DEPTH, 3]); kvnormT = IN("kvnormT", [128, DEPTH, 2])
    kv_norm = IN("kv_norm", [DEPTH, 256]); pscaleT = IN("pscaleT", [128, DEPTH, 4])
    w_uq = IN("w_uq", [DEPTH, 384, 1024]); w_ukv = IN("w_ukv", [DEPTH, 256, 1024])
    w_pool = IN("w_pool", [DEPTH, 4, 128, 128]); w_out = IN("w_out", [DEPTH, D, D])
    masks_d = IN("masks", [128, 8, 128]); sel_d = IN("sel", [128, 2, 128]); rope_d = IN("rope", [64, 2, TS]); corr_d = IN("corr", [4, 16])
    yp = OUT("yp", [NP * TP, D]); ys = OUT("ys", [TS, D])
    nckv = OUT("nckv", [NP, DEPTH, TP, 256]); nkpe = OUT("nkpe", [NP, DEPTH, TP, 64])
    nsf = OUT("nsf", [NP, DEPTH, 8, 128, 128]); nsb = OUT("nsb", [NP, DEPTH, 8, 128, 128])
    dbg = cfg.get("dbg", ())
    SCR = lambda n, s, dt: cx.dram(n, s, dt, kind=("ExternalOutput" if n in dbg else None))
    hT_d = SCR("hT_d", [D, NTOK], BF16)
    projT_d = SCR("projT_d", [NFM * 128, NTOK], BF16)
    ab_d = SCR("ab_d", [NTOK, 32], F32)
    mixT_d = SCR("mixT_d", [D, NTOK], BF16)
    qk_d = SCR("qk_d", [8, 2, 128, NTOK], BF16)
    ktm_d = SCR("ktm_d", [NTOK, 1024], BF16)
    vtm_d = SCR("vtm_d", [NTOK, 1024], BF16)
    of_d = SCR("of_d", [1024, NTOK], BF16)
    ob_d = SCR("ob_d", [1024, NTOK], BF16)

    ident_f = cx.sb("ident_f", [128, 128], F32)
    ident_b = cx.sb("ident_b", [128, 128], BF16)
    ones_f = cx.sb("ones_f", [128, 128], F32)
    ones_b = cx.sb("ones_b", [128, 128], BF16)
    masks = cx.sb("masks", [128, 8, 128], F32)
    scT = cx.sb("scT", [128, 16, 2], F32)
    cx.dma("sp", masks[:], masks_d[:])
    cx.dma("sp", scT[:], cT[:])
    cx.copy(ident_f[:], masks[:, 7, :])
    cx.copy(ident_b[:], masks[:, 7, :])
    cx.memset(ones_f[:], 1.0)
    cx.memset(ones_b[:], 1.0)
    cx.act(scT[:], scT[:], AF.Silu)
    PS = [cx.ps("ps%d" % i, [128, 512], F32) for i in range(8)]
    modbc = cx.sb("modbc", [128, 2, 3, D], BF16)

    groups = []
    if cfg.get("do_prompt", True):
        groups.append(dict(name="p", r=0, tok0=0, nseq=NP, T=TP, ctx=False))
    if cfg.get("do_sample", True):
        groups.append(dict(name="s", r=1, tok0=NP * TP, nseq=1, T=TS, ctx=True))

    for l in range(depth):
        last = (l == depth - 1)
        cx.mark("L%d Phase 0" % l)
        with cx.scope():
            lb = cx.sb("lb", [128, 2, 16, 128], F32)
            for r in range(2):
                for k in range(16):
                    cx.ts(lb[:, r, k, :], ones_f[:], scT[:, k, r:r + 1], ALU.mult)
            wA = [cx.sb("wA%d" % i, [128, 16, 512], F32) for i in range(2)]
            bb = [cx.sb("bb%d" % i, [128, 512], F32) for i in range(2)]
            nb = [cx.sb("nb%d" % i, [128, 512], F32) for i in range(2)]
            tmp = cx.sb("p0tmp", [128, 512], F32)
            for nt in range(12):
                part, c0 = nt // 4, (nt % 4) * 512
                w_, b_, n_ = wA[nt % 2], bb[nt % 2], nb[nt % 2]
                cx.dma("sp", w_[:], w_ada[l, :, nt * 512:(nt + 1) * 512].rr("(k p) n -> p k n", p=128))
                cx.dma("sp", b_[:], b_ada[l, nt * 512:(nt + 1) * 512].pbc())
                if part == 1:
                    cx.dma("sp", n_[:], norm_pre[l, c0:c0 + 512].pbc())
                elif part == 2:
                    cx.dma("sp", n_[:], norm_post[l, c0:c0 + 512].pbc())
                for r in range(2):
                    ps = PS[(nt * 2 + r) % 4]
                    for k in range(16):
                        cx.mm(ps[:], lb[:, r, k, :], w_[:, k, :], start=(k == 0), stop=(k == 15))
                    if part == 0:
                        cx.tt(modbc[:, r, 1, c0:c0 + 512], ps[:], b_[:], ALU.add)
                    elif part == 1:
                        cx.stt(tmp[:], ps[:], 1.0, b_[:], ALU.add, ALU.add)
                        cx.tt(modbc[:, r, 0, c0:c0 + 512], tmp[:], n_[:], ALU.mult)
                    else:
                        cx.tt(tmp[:], ps[:], b_[:], ALU.add)
                        cx.tt(modbc[:, r, 2, c0:c0 + 512], tmp[:], n_[:], ALU.mult)
        if stop_after == "p0":
            continue
        cx.mark("L%d Phase 1a" % l)
        with cx.scope():
            xt = [cx.sb("xt%d" % i, [128, D], F32) for i in range(2)]
            junk = cx.sb("junk", [128, D], BF16)
            t1 = cx.sb("t1", [128, D], F32)
            hb = [cx.sb("hb%d" % i, [128, D], BF16) for i in range(2)]
            ss = cx.sb("ss", [128, 4], F32)
            hTt = [cx.sb("hTt%d" % i, [128, 16, 512], BF16) for i in range(2)]
            PSB = [PS[6 + i][:].bitcast(BF16).rr("p (k t) -> p k t", k=8) for i in range(2)]
            for g in groups:
                xin = (xp if g["name"] == "p" else xs) if l == 0 else (yp if g["name"] == "p" else ys)
                ntile = g["nseq"] * g["T"] // 128
                for ti in range(ntile):
                    x_ = xt[ti % 2]
                    h_ = hb[ti % 2]
                    hT_ = hTt[(ti // 4) % 2]
                    cx.dma("sp", x_[:], xin[ti * 128:(ti + 1) * 128, :])
                    cx.act(junk[:], x_[:], AF.Square, accum=ss[:, 0:1])
                    cx.act(ss[:, 1:2], ss[:, 0:1], AF.Ln, scale=1.0 / D, bias=EPS)
                    cx.act(ss[:, 2:3], ss[:, 1:2], AF.Exp, scale=-0.5)
                    cx.stt(t1[:], x_[:], ss[:, 2:3], modbc[:, g["r"], 0, :], ALU.mult, ALU.mult)
                    cx.tt(h_[:], t1[:], modbc[:, g["r"], 1, :], ALU.add)
                    for half in range(2):
                        pb = PSB[(ti * 2 + half) % 2]
                        for kk in range(8):
                            k = half * 8 + kk
                            cx.tr(pb[:, kk, :], h_[:, k * 128:(k + 1) * 128], ident_b[:])
                        cx.copy(hT_[:, half * 8:(half + 1) * 8, (ti % 4) * 128:(ti % 4 + 1) * 128], pb[:], eng=("act" if half else "dve"))
                    if ti % 4 == 3 or ti == ntile - 1:
                        n = (ti % 4 + 1) * 128
                        t0 = g["tok0"] + (ti // 4) * 512
                        cx.dma("pool", hT_d[:, t0:t0 + n].rr("(k p) t -> p k t", p=128), hT_[:, :, 0:n])
        if stop_after == "p1a":
            continue
        cx.mark("L%d Phase 1b" % l)
        with cx.scope():
            WMAX = max(ch - cl for cl, ch in PASSES) * 128
            wPs = [cx.sb("wP%d" % i, [128, 16, WMAX], BF16) for i in range(2)]
            wT = cx.sb("wT", [128, 16, NTM], BF16)
            kvn_bc = cx.sb("kvn_bc", [128, 256], F32)
            tmo = [cx.sb("tmo%d" % i, [128, NTM], F32) for i in range(2)]
            cko = [cx.sb("cko%d" % i, [128, 256], F32) for i in range(2)]
            st = cx.sb("st1b", [128, 4], F32)
            junk2 = cx.sb("junk2", [128, 256], BF16)
            hTt = [cx.sb("hTt%d" % i, [128, 16, 512], BF16) for i in range(2)]
            stage = [cx.sb("stage%d" % i, [128, 4, 512], BF16) for i in range(2)]

            def load_w(pi):
                cl, ch = PASSES[pi]
                for k in range(16):
                    cx.dma("pool", wPs[pi % 2][:, k, 0:(ch - cl) * 128], w_in_fm[l, k * 128:(k + 1) * 128, cl * 128:ch * 128])
            load_w(0)
            for k in range(16):
                cx.dma("pool", wT[:, k, :], w_in_tm[l, k * 128:(k + 1) * 128, :])
            cx.dma("sp", kvn_bc[:], kv_norm[l, :].pbc())
            ev = 0
            sti = 0
            hti = 0
            for pi, (c_lo, c_hi) in enumerate(PASSES):
                wP = wPs[pi % 2]
                if pi + 1 < len(PASSES):
                    load_w(pi + 1)
                for g in groups:
                    ntt = g["nseq"] * g["T"] // 512
                    for tt in range(ntt):
                        t0 = g["tok0"] + tt * 512
                        hT_ = hTt[hti % 2]
                        hti += 1
                        cx.dma("sp", hT_[:], hT_d[:, t0:t0 + 512].rr("(k p) t -> p k t", p=128))
                        if pi == 0:
                            for s_ in range(4):
                                ps = PS[4 + s_ % 2]
                                for k in range(16):
                                    cx.mm(ps[:, 0:NTM], hT_[:, k, s_ * 128:(s_ + 1) * 128], wT[:, k, :], start=(k == 0), stop=(k == 15))
                                o_ = tmo[s_ % 2]
                                cx.copy(o_[:], ps[:, 0:NTM], eng="act")
                                tk = t0 + s_ * 128
                                cx.dma("pool", ab_d[tk:tk + 128, :], o_[:, 0:32])
                                if not g["ctx"]:
                                    b_, tp = (tk - g["tok0"]) // TP, (tk - g["tok0"]) % TP
                                    cx.dma("pool", nkpe[b_, l, tp:tp + 128, :], o_[:, 288:352])
                                    ck = cko[s_ % 2]
                                    cx.act(junk2[:], o_[:, 32:288], AF.Square, accum=st[:, 0:1])
                                    cx.act(st[:, 1:2], st[:, 0:1], AF.Ln, scale=1.0 / 256, bias=EPS)
                                    cx.act(st[:, 2:3], st[:, 1:2], AF.Exp, scale=-0.5)
                                    cx.stt(ck[:], o_[:, 32:288], st[:, 2:3], kvn_bc[:], ALU.mult, ALU.mult)
                                    cx.dma("pool", nckv[b_, l, tp:tp + 128, :], ck[:])
                        for c in range(c_lo, c_hi):
                            ci = c - c_lo
                            ps = PS[ci % 4]
                            for k in range(16):
                                cx.mm(ps[:], wP[:, k, ci * 128:(ci + 1) * 128], hT_[:, k, :], start=(k == 0), stop=(k == 15))
                            sg = stage[sti % 2]
                            cx.copy(sg[:, ci % 4, :], ps[:], eng=("act" if ev % 2 else "dve"))
                            ev += 1
                            if ci % 4 == 3 or c == c_hi - 1:
                                n = ci % 4 + 1
                                cb = c - n + 1
                                cx.dma("pool", projT_d[cb * 128:(cb + n) * 128, t0:t0 + 512].rr("(c p) t -> p c t", p=128), sg[:, 0:n, :])
                                sti += 1
        if stop_after == "p1b":
            continue
        cx.mark("L%d Phase 2a" % l)
        with cx.scope():
            cw = cx.sb("cw", [128, 24, 5], F32)
            cx.dma("sp", cw[:], convT[:, l, :, :])
            dg = cx.sb("dg", [128, 24, 5, 128], BF16)
            for c in range(24):
                for k in range(5):
                    cx.ts(dg[:, c, k, :], ident_f[:], cw[:, c, k:k + 1], ALU.mult)
            PSB = [PS[6 + i][:].bitcast(BF16).rr("p (k t) -> p k t", k=8) for i in range(2)]
            xin = [cx.sb("xin%d" % i, [128, 516], BF16) for i in range(3)]
            sil = cx.sb("sil", [128, 16, 512], F32)
            silv = cx.sb("silv", [128, 8, 512], BF16)
            sq = [cx.sb("sq%d" % i, [128, 512], BF16) for i in range(2)]
            rs = [cx.sb("rs%d" % i, [128, 512], F32) for i in range(2)]
            qkst = cx.sb("qkst", [128, 8, 2, 512], BF16)
            tmst = [cx.sb("tmst%d" % i, [128, 4, 1024], BF16) for i in range(2)]
            it = 0
            for g in groups:
                T = g["T"]
                BLK = min(T, 512)
                for sidx in range(g["nseq"]):
                    for bi in range(T // BLK):
                        tb = bi * BLK
                        t0 = g["tok0"] + sidx * T + tb
                        nsub = BLK // 128
                        for c in range(24):
                            x_ = xin[it % 3]
                            lo = 2 if tb == 0 else 0
                            hi = 2 if tb + BLK == T else 0
                            if lo:
                                cx.memset(x_[:, 0:2], 0.0, eng="pool")
                            if hi:
                                cx.memset(x_[:, BLK + 2:BLK + 4], 0.0, eng="pool")
                            cx.dma("sp", x_[:, lo:BLK + 4 - hi], projT_d[c * 128:(c + 1) * 128, t0 - 2 + lo:t0 + BLK + 2 - hi])
                            ps = PS[it % 4]
                            it += 1
                            for k in range(5):
                                cx.mm(ps[:, 0:BLK], dg[:, c, k, :], x_[:, k:k + BLK], start=(k == 0), stop=(k == 4))
                            if c < 16:
                                cx.act(sil[:, c, 0:BLK], ps[:, 0:BLK], AF.Silu)
                            else:
                                cx.act(silv[:, c - 16, 0:BLK], ps[:, 0:BLK], AF.Silu)
                        for kind in range(3):
                            for h in range(8):
                                c = kind * 8 + h
                                if kind < 2:
                                    q_, r_ = sq[it % 2], rs[it % 2]
                                    cx.act(q_[:, 0:BLK], sil[:, c, 0:BLK], AF.Square)
                                    ps2 = PS[4 + it % 2]
                                    cx.mm(ps2[:, 0:BLK], ones_b[:], q_[:, 0:BLK])
                                    cx.act(r_[:, 0:BLK], ps2[:, 0:BLK], AF.Ln, bias=EPS)
                                    cx.act(r_[:, 0:BLK], r_[:, 0:BLK], AF.Exp, scale=-0.5)
                                    if kind == 0:
                                        cx.stt(qkst[:, h, 1, 0:BLK], sil[:, c, 0:BLK], 128.0 ** -0.5, r_[:, 0:BLK], ALU.mult, ALU.mult)
                                    else:
                                        cx.tt(qkst[:, h, 0, 0:BLK], sil[:, c, 0:BLK], r_[:, 0:BLK], ALU.mult)
                                if kind >= 1:
                                    src = silv[:, h, 0:BLK] if kind == 2 else qkst[:, h, 0, 0:BLK]
                                    tm = tmst[kind - 1]
                                    pb = PSB[it % 2]
                                    for s_i in range(nsub):
                                        cx.tr(pb[:, s_i, :], src[:, s_i * 128:(s_i + 1) * 128], ident_b[:])
                                    cx.copy(tm[:, 0:nsub, h * 128:(h + 1) * 128], pb[:, 0:nsub, :], eng=("act" if it % 2 else "dve"))
                                it += 1
                            if kind >= 1:
                                dst = ktm_d if kind == 1 else vtm_d
                                cx.dma("pool", dst[t0:t0 + BLK, :].rr("(s p) c -> p s c", p=128), tmst[kind - 1][:, 0:nsub, :])
                        cx.dma("pool", qk_d[:, :, :, t0:t0 + BLK].rr("h kq p t -> p h kq t"), qkst[:, :, :, 0:BLK])
        if stop_after == "p2a":
            continue
        cx.mark("L%d Phase 2b" % l)
        with cx.scope():
            NUMAX = 32
            f32r_ok = True
            ident_r = cx.sb("ident_r", [128, 128], F32R)
            cx.copy(ident_r[:], ident_f[:])
            selt = cx.sb("selt", [128, 2, 128], F32)
            cx.dma("sp", selt[:], sel_d[:])
            dtb = cx.sb("dtb", [128, 16], F32)
            nega = cx.sb("nega", [128, 16], F32)
            cx.dma("sp", dtb[:], dt_bias[l, :].pbc())
            cx.dma("sp", nega[:], a_log[l, :].pbc())
            cx.act(nega[:], nega[:], AF.Exp)
            cx.ts(nega[:], nega[:], -1.0, ALU.mult)
            qkT = cx.sb("qkT", [128, 8, 2, 512], BF16)
            ktm = cx.sb("ktm", [128, 4, 1024], BF16)
            vtm = cx.sb("vtm", [128, 4, 1024], BF16)
            abt = cx.sb("abt", [128, 4, 32], F32)
            gful = cx.sb("gful", [128, 4, 16], F32)
            bful = cx.sb("bful", [128, 4, 16], F32)
            gd = cx.sb("gd", [128, NUMAX], F32)
            gc = cx.sb("gc", [128, NUMAX], F32)
            tot = cx.sb("tot", [128, NUMAX], F32)
            egc = cx.sb("egc", [128, NUMAX], F32)
            ekg = cx.sb("ekg", [128, NUMAX], F32)
            nbeta = cx.sb("nbeta", [128, NUMAX], F32)
            gam = cx.sb("gam", [128, 2, NUMAX], F32)
            GS = 4
            ugS = [cx.sb("ugS%d" % i, [128, 4, 128], F32) for i in range(GS)]
            ztS = [cx.sb("ztS%d" % i, [128, 4, 128], F32) for i in range(GS)]
            esS = [cx.sb("esS%d" % i, [128, 4, 128], F32) for i in range(GS)]
            egS = [cx.sb("egS%d" % i, [128, 4, 128], BF16) for i in range(GS)]
            ZS = [cx.sb("ZS%d" % i, [128, 4, 128], F32) for i in range(GS)]
            R32 = [cx.sb("R32_%d" % i, [128, 4, 128], F32) for i in range(GS)]
            Zh = [cx.sb("Zh%d" % i, [128, 4, 128], BF16) for i in range(GS)]
            Zl = [cx.sb("Zl%d" % i, [128, 4, 128], BF16) for i in range(GS)]
            ZTh = [cx.sb("ZTh%d" % i, [128, 4, 128], BF16) for i in range(GS)]
            ZTl = [cx.sb("ZTl%d" % i, [128, 4, 128], BF16) for i in range(GS)]
            Rh = [cx.sb("Rh%d" % i, [128, 4, 128], BF16) for i in range(GS)]
            Rl = [cx.sb("Rl%d" % i, [128, 4, 128], BF16) for i in range(GS)]
            AttnT = cx.sb("AttnT", [128, NUMAX, 128], BF16)
            QgT = cx.sb("QgT", [128, NUMAX, 128], BF16)
            Rf = cx.sb("Rf", [128, NUMAX, 128], BF16)
            Kg = cx.sb("Kg", [128, NUMAX, 128], BF16)
            Kg1 = cx.sb("Kg1", [128, NUMAX, 128], BF16)
            ekg0 = cx.sb("ekg0", [128, NUMAX], F32)
            ekg1 = cx.sb("ekg1", [128, NUMAX], F32)
            S32 = [cx.sb("S32_%d" % h, [128, 128], F32) for h in range(8)]
            Sbf = [cx.sb("Sbf_%d" % h, [128, 128], BF16) for h in range(8)]
            Xt = [cx.sb("Xt_%d" % h, [128, 128], BF16) for h in range(8)]
            Vn = [cx.sb("Vn_%d" % h, [128, 128], BF16) for h in range(8)]
            Ost = [cx.sb("Ost%d" % i, [128, 8, 512], BF16) for i in range(2)]
            for h in range(8):
                cx.memset(Xt[h][:], 0.0, eng="pool")
                cx.memset(Vn[h][:], 0.0, eng="pool")
            psKS = [PS[h].sub("psKS%d" % h) for h in range(8)]
            psAX = [PS[h].sub("psAX%d" % h) for h in range(8)]
            psO = [PS[h].sub("psO%d" % h) for h in range(8)]
            psDS = [PS[h].sub("psDS%d" % h) for h in range(8)]
            ident4 = ident_f[:].unsq(1).bcast([128, 4, 128])
            blki = 0
            for g in groups:
                T = g["T"]
                BLK = min(T, 512)
                nblk = T // BLK
                nt = BLK // 128
                NU = nt * 8
                for sidx in range(g["nseq"]):
                    for d in range(2):
                        od_d = of_d if d == 0 else ob_d
                        for h in range(8):
                            if g["ctx"]:
                                cx.dma("sp", S32[h][:], (s0f if d == 0 else s0b)[l, h, :, :])
                            else:
                                cx.memset(S32[h][:], 0.0, eng="pool")
                            cx.copy(Sbf[h][:], S32[h][:], eng="pool")
                        for bo in range(nblk):
                            bi = bo if d == 0 else nblk - 1 - bo
                            t0 = g["tok0"] + sidx * T + bi * BLK
                            cx.dma("sp", qkT[:, :, :, 0:BLK], qk_d[:, :, :, t0:t0 + BLK].rr("h kq p t -> p h kq t"))
                            cx.dma("sp", ktm[:, 0:nt, :], ktm_d[t0:t0 + BLK, :].rr("(s p) c -> p s c", p=128))
                            cx.dma("sp", vtm[:, 0:nt, :], vtm_d[t0:t0 + BLK, :].rr("(s p) c -> p s c", p=128))
                            cx.dma("sp", abt[:, 0:nt, :], ab_d[t0:t0 + BLK, :].rr("(s p) c -> p s c", p=128))
                            if cfg.get("p2b_stop", 99) <= 1:
                                continue
                            cx.mark("L%d 2b-ld" % l)
                            for s in range(nt):
                                cx.tt(gful[:, s, :], abt[:, s, 16:32], dtb[:], ALU.add)
                            cx.act(gful[:, 0:nt, :], gful[:, 0:nt, :], AF.Exp)
                            cx.act(gful[:, 0:nt, :], gful[:, 0:nt, :], AF.Ln, bias=1.0)
                            for s in range(nt):
                                cx.tt(gful[:, s, :], gful[:, s, :], nega[:], ALU.mult)
                            cx.act(bful[:, 0:nt, :], abt[:, 0:nt, 0:16], AF.Exp, scale=-1.0)
                            cx.ts(bful[:, 0:nt, :], bful[:, 0:nt, :], 1.0, ALU.add)
                            cx.recip(bful[:, 0:nt, :], bful[:, 0:nt, :])
                            if cfg.get("p2b_stop", 99) <= 2:
                                continue
                            gdv = gd[:, 0:NU].rr("p (s h) -> p s h", h=8)
                            cx.copy(gdv, gful[:, 0:nt, d * 8:(d + 1) * 8])
                            cx.ts(nbeta[:, 0:NU].rr("p (s h) -> p s h", h=8), bful[:, 0:nt, d * 8:(d + 1) * 8], -1.0, ALU.mult)
                            pz = PS[7]
                            cx.mm(pz[:, 0:NU], masks[:, d, :], gd[:, 0:NU])
                            cx.mm(pz[:, 32:32 + NU], masks[:, 6, :], gd[:, 0:NU])
                            cx.mm(pz[:, 64:64 + NU], selt[:, 0, :], gd[:, 0:NU])
                            cx.mm(pz[:, 96:96 + NU], selt[:, 1, :], gd[:, 0:NU])
                            cx.copy(gc[:, 0:NU], pz[:, 0:NU])
                            cx.act(egc[:, 0:NU], pz[:, 0:NU], AF.Exp)
                            cx.tt(tot[:, 0:NU], pz[:, 32:32 + NU], gc[:, 0:NU], ALU.subtract)
                            cx.act(ekg[:, 0:NU], tot[:, 0:NU], AF.Exp)
                            cx.ts(ekg0[:, 0:NU], ekg[:, 0:NU], selt[:, 0, 0:1], ALU.mult)
                            cx.ts(ekg1[:, 0:NU], ekg[:, 0:NU], selt[:, 1, 0:1], ALU.mult)
                            cx.act(gam[:, 0, 0:NU], pz[:, 64:64 + NU], AF.Exp)
                            cx.act(gam[:, 1, 0:NU], pz[:, 96:96 + NU], AF.Exp)
                            cx.mark("L%d 2b-pre" % l)
                            ngrp = NU // 4
                            for sg in range(0, ngrp, GS):
                                qs = list(range(min(GS, ngrp - sg)))
                                n0s = [(sg + q) * 4 for q in qs]
                                for q in qs:
                                    n0 = n0s[q]
                                    cx.tt(ugS[q][:], masks[:, d, :].unsq(1).bcast([128, 4, 128]), gd[:, n0:n0 + 4].unsq(2).bcast([128, 4, 128]), ALU.mult, eng="pool")
                                for q in qs:
                                    cx.mm(PS[q][:], ones_f[:], ugS[q][:].rr("p a b -> p (a b)"))
                                for q in qs:
                                    n0 = n0s[q]
                                    tt_, h0 = n0 // 8, n0 % 8
                                    pG3 = PS[q][:].rr("p (a b) -> p a b", a=4)
                                    cx.act(egS[q][:].rr("p a b -> p (a b)"), PS[q][:], AF.Exp)
                                    cx.tt(ztS[q][:], pG3, gc[:, n0:n0 + 4].unsq(2).bcast([128, 4, 128]), ALU.subtract)
                                    cx.tt(ztS[q][:], ztS[q][:], masks[:, 4 + d, :].unsq(1).bcast([128, 4, 128]), ALU.add, eng="pool")
                                    cx.tt(QgT[:, n0:n0 + 4, :], qkT[:, h0:h0 + 4, 1, tt_ * 128:(tt_ + 1) * 128], egS[q][:], ALU.mult, eng="pool")
                                    cx.tt(Kg[:, n0:n0 + 4, :], ktm[:, tt_, h0 * 128:(h0 + 4) * 128].rr("p (a b) -> p a b", a=4),
                                          ekg0[:, n0:n0 + 4].unsq(2).bcast([128, 4, 128]), ALU.mult, eng="pool")
                                    cx.tt(Kg1[:, n0:n0 + 4, :], ktm[:, tt_, h0 * 128:(h0 + 4) * 128].rr("p (a b) -> p a b", a=4),
                                          ekg1[:, n0:n0 + 4].unsq(2).bcast([128, 4, 128]), ALU.mult, eng="pool")
                                for q in qs:
                                    n0 = n0s[q]
                                    tt_, h0 = n0 // 8, n0 % 8
                                    cx.act(ztS[q][:].rr("p a b -> p (a b)"), ztS[q][:].rr("p a b -> p (a b)"), AF.Exp)
                                    cx.tt(esS[q][:], ztS[q][:], masks[:, 2 + d, :].unsq(1).bcast([128, 4, 128]), ALU.mult, eng="pool")
                                    cx.tt(esS[q][:], esS[q][:], nbeta[:, n0:n0 + 4].unsq(2).bcast([128, 4, 128]), ALU.mult, eng="pool")
                                    for j in range(4):
                                        h = h0 + j
                                        pP = PS[4 + q] if j < 2 else PS[q]
                                        cx.mm(pP[:, (j % 2) * 256:(j % 2 + 1) * 256].rr("p (a b) -> p a b", a=2), qkT[:, h, 0, tt_ * 128:(tt_ + 1) * 128],
                                              qkT[:, h, :, tt_ * 128:(tt_ + 1) * 128])
                                for q in qs:
                                    n0 = n0s[q]
                                    for jj in range(2):
                                        pP = PS[4 + q] if jj == 0 else PS[q]
                                        cx.tt(ZS[q][:, 2 * jj:2 * jj + 2, :], pP[:].rr("p (a b c) -> p a b c", a=2, b=2)[:, :, 0, :], esS[q][:, 2 * jj:2 * jj + 2, :], ALU.mult)
                                        cx.tt(AttnT[:, n0 + 2 * jj:n0 + 2 * jj + 2, :], pP[:].rr("p (a b c) -> p a b c", a=2, b=2)[:, :, 1, :], ztS[q][:, 2 * jj:2 * jj + 2, :], ALU.mult)
                                if cfg.get("skip_inv"):
                                    continue
                                def split(q, hi, lo, ps):
                                    cx.copy(hi[:].rr("p a b -> p (a b)"), ps[:], eng="act")
                                    cx.tt(lo[:].rr("p a b -> p (a b)"), ps[:], hi[:].rr("p a b -> p (a b)"), ALU.subtract)

                                def prod(ps, A, B, j):
                                    o = ps[:, j * 128:(j + 1) * 128]
                                    cx.mm(o, A[0][:, j, :], B[0][:, j, :], start=True, stop=False)
                                    cx.mm(o, A[0][:, j, :], B[1][:, j, :], start=False, stop=False)
                                    cx.mm(o, A[1][:, j, :], B[0][:, j, :], start=False, stop=True)
                                for q in qs:
                                    cx.copy(Zh[q][:], ZS[q][:], eng="act")
                                    cx.tt(Zl[q][:], ZS[q][:], Zh[q][:], ALU.subtract, eng="pool")
                                    cx.tt(R32[q][:], ZS[q][:], ident4, ALU.add)
                                for q in qs:
                                    for j in range(4):
                                        o = PS[q][:, j * 128:(j + 1) * 128]
                                        cx.mm(o, Zh[q][:, j, :], ident_b[:], start=True, stop=False)
                                        cx.mm(o, Zl[q][:, j, :], ident_b[:], start=False, stop=True)
                                for q in qs:
                                    split(q, ZTh[q], ZTl[q], PS[q])
                                    cx.copy(Rh[q][:], R32[q][:], eng="act")
                                    cx.tt(Rl[q][:], R32[q][:], Rh[q][:], ALU.subtract, eng="pool")
                                for kk in range(1, 6):
                                    for q in qs:
                                        for j in range(4):
                                            prod(PS[q], (Zh[q], Zl[q]), (ZTh[q], ZTl[q]), j)
                                        if kk < 5:
                                            for j in range(4):
                                                prod(PS[4 + q], (ZTh[q], ZTl[q]), (Zh[q], Zl[q]), j)
                                    for q in qs:
                                        split(q, ZTh[q], ZTl[q], PS[q])
                                        if kk < 5:
                                            split(q, Zh[q], Zl[q], PS[4 + q])
                                    for q in qs:
                                        for j in range(4):
                                            prod(PS[q], (ZTh[q], ZTl[q]), (Rh[q], Rl[q]), j)
                                    for q in qs:
                                        n0 = n0s[q]
                                        if kk < 5:
                                            cx.tt(R32[q][:].rr("p a b -> p (a b)"), PS[q][:], R32[q][:].rr("p a b -> p (a b)"), ALU.add)
                                            cx.copy(Rh[q][:], R32[q][:], eng="act")
                                            cx.tt(Rl[q][:], R32[q][:], Rh[q][:], ALU.subtract, eng="pool")
                                        else:
                                            cx.tt(Rf[:, n0:n0 + 4, :].rr("p a b -> p (a b)"), PS[q][:], R32[q][:].rr("p a b -> p (a b)"), ALU.add)
                            cx.mark("L%d 2b-scan" % l)
                            ost = Ost[blki % 2]
                            blki += 1
                            nch = BLK // 64
                            for co in range(0 if cfg.get("skip_scan") else nch):
                                c = co if d == 0 else nch - 1 - co
                                tt_, s = c // 2, c % 2
                                r0_, r1_ = s * 64, s * 64 + 64
                                cs = slice(c * 64, c * 64 + 64)
                                ns = [tt_ * 8 + h for h in range(8)]
                                tl = slice(tt_ * 128, tt_ * 128 + 128)
                                for h in range(8):
                                    cx.mm(psKS[h][:, 0:128], qkT[:, h, 0, tl], Sbf[h][:])
                                for h in range(8):
                                    n = ns[h]
                                    cx.stt(Xt[h][r0_:r1_, :], psKS[h][r0_:r1_, 0:128], egc[r0_:r1_, n:n + 1], vtm[r0_:r1_, tt_, h * 128:(h + 1) * 128], ALU.mult, ALU.subtract)
                                for h in range(8):
                                    cx.mm(psAX[h][:, 128:256], Rf[:, ns[h], :], Xt[h][:])
                                for h in range(8):
                                    n = ns[h]
                                    cx.act(Vn[h][r0_:r1_, :], psAX[h][r0_:r1_, 128:256], AF.Copy, scale=nbeta[r0_:r1_, n:n + 1])
                                for h in range(8):
                                    n = ns[h]
                                    cx.mm(psO[h][:, 256:320], Sbf[h][:], QgT[:, n, r0_:r1_], start=True, stop=False)
                                    cx.mm(psO[h][:, 256:320], Vn[h][:], AttnT[:, n, r0_:r1_], start=False, stop=True)
                                    cx.mm(psDS[h][:, 320:448], (Kg if s == 0 else Kg1)[:, n, :], Vn[h][:])
                                for h in range(8):
                                    n = ns[h]
                                    cx.stt(S32[h][:], S32[h][:], gam[:, s, n:n + 1], psDS[h][:, 320:448], ALU.mult, ALU.add)
                                for h in range(8):
                                    cx.copy(Sbf[h][:], S32[h][:], eng="pool")
                                    cx.copy(ost[:, h, cs], psO[h][:, 256:320], eng="act")
                            cx.dma("sp", od_d[:, t0:t0 + BLK].rr("(h p) t -> p h t", p=128), ost[:, :, 0:BLK])
                        if not g["ctx"]:
                            b = sidx
                            for h in range(8):
                                cx.dma("sp", (nsf if d == 0 else nsb)[b, l, h, :, :], S32[h][:])
        if stop_after == "p2b":
            continue
        cx.mark("L%d Phase 2c" % l)
        with cx.scope():
            onc_all = cx.sb("onc", [128, DEPTH], F32)
            cx.dma("sp", onc_all[:], onormT[:, :])
            onc = onc_all[:, l:l + 1]
            oft = [cx.sb("oft%d" % i, [128, 512], BF16) for i in range(2)]
            obt = [cx.sb("obt%d" % i, [128, 512], BF16) for i in range(2)]
            zat8 = cx.sb("zat8", [128, 8, 512], BF16)
            sz8 = cx.sb("sz8", [128, 8, 512], BF16)
            osum = [cx.sb("osum%d" % i, [128, 512], F32) for i in range(2)]
            osq = [cx.sb("osq%d" % i, [128, 512], BF16) for i in range(2)]
            rsd = [cx.sb("rsd%d" % i, [128, 512], F32) for i in range(2)]
            mst = [cx.sb("mst%d" % i, [128, 8, 512], BF16) for i in range(2)]
            it = 0
            for g in groups:
                ntt = g["nseq"] * g["T"] // 512
                for tt in range(ntt):
                    t0 = g["tok0"] + tt * 512
                    ms = mst[tt % 2]
                    for h in range(8):
                        cx.dma("sp", zat8[:, h, :], projT_d[(CH_ZA + h) * 128:(CH_ZA + h + 1) * 128, t0:t0 + 512])
                    for h in range(8):
                        cx.act(sz8[:, h, :], zat8[:, h, :], AF.Silu)
                    for h in range(8):
                        i2 = it % 2
                        it += 1
                        cx.dma("sp", oft[i2][:], of_d[h * 128:(h + 1) * 128, t0:t0 + 512])
                        cx.dma("sp", obt[i2][:], ob_d[h * 128:(h + 1) * 128, t0:t0 + 512])
                        cx.tt(osum[i2][:], oft[i2][:], obt[i2][:], ALU.add, eng="pool")
                        cx.act(osq[i2][:], osum[i2][:], AF.Square)
                        ps = PS[it % 4]
                        cx.mm(ps[:], ones_b[:], osq[i2][:])
                        cx.act(rsd[i2][:], ps[:], AF.Ln, scale=1.0 / 128, bias=EPS)
                        cx.act(rsd[i2][:], rsd[i2][:], AF.Exp, scale=-0.5)
                        cx.stt(osum[i2][:], osum[i2][:], onc, rsd[i2][:], ALU.mult, ALU.mult)
                        cx.tt(ms[:, h, :], osum[i2][:], sz8[:, h, :], ALU.mult)
                    cx.dma("pool", mixT_d[0:1024, t0:t0 + 512].rr("(h p) t -> p h t", p=128), ms[:])
        if stop_after == "p2c":
            continue
        cx.mark("L%d Phase 3" % l)
        with cx.scope():
            NKMAX = PAST + TS
            wuq = cx.sb("wuq", [128, 3, 1024], BF16)
            wukv = cx.sb("wukv", [128, 2, 1024], BF16)
            for c in range(3):
                cx.dma("pool", wuq[:, c, :], w_uq[l, c * 128:(c + 1) * 128, :])
            for c in range(2):
                cx.dma("pool", wukv[:, c, :], w_ukv[l, c * 128:(c + 1) * 128, :])
            qnc = cx.sb("qnc", [128, 3], F32)
            kvc = cx.sb("kvc", [128, 2], F32)
            cx.dma("sp", qnc[:], qnormT[:, l, :])
            cx.dma("sp", kvc[:], kvnormT[:, l, :])
            ckvnT = cx.sb("ckvnT", [128, 2, NKMAX], BF16)
            KT = cx.sb("KT", [128, 4, NKMAX], BF16)
            KPT = cx.sb("KPT", [128, NKMAX], BF16)
            cx.memset(KPT[64:128, :], 0.0, eng="pool")
            Vtm = cx.sb("Vtm", [128, NKMAX // 128, 512], BF16)
            QT = [cx.sb("QT", [128, 4, 2, 512], BF16)] * 2
            cx.memset(QT[0][64:128, :, 1, :], 0.0, eng="pool")
            rawT = [cx.sb("rawT", [128, 3, 512], BF16)] * 2
            sqT = [cx.sb("sqT", [128, 3, 512], BF16)] * 2
            rsm = [cx.sb("rsm", [128, 512], F32)] * 2
            cqn = [cx.sb("cqn", [128, 3, 512], BF16)] * 2
            ropet = [cx.sb("ropet", [64, 2, 512], F32)] * 2
            kpr = [cx.sb("kpr", [64, 2, 512], BF16)] * 2
            rt1 = [cx.sb("rt1", [64, 512], F32)] * 2
            rt2 = [cx.sb("rt2", [64, 512], F32)] * 2
            PT = [cx.sb("PT%d" % i, [128, 512], BF16) for i in range(3)]
            rden = [cx.sb("rden%d" % i, [128, 512], F32) for i in range(2)]
            zbt4 = cx.sb("zbt4", [128, 4, 512], BF16)
            accP = [cx.sb("accP%d" % i, [128, 512], F32) for i in range(2)]
            gzb4 = cx.sb("gzb4", [128, 4, 512], BF16)
            obst = [cx.sb("obst%d" % i, [128, 4, 512], BF16) for i in range(2)]
            cst = cx.sb("cst", [128, 2, 256], F32)
            cstb = cx.sb("cstb", [128, 2, 256], BF16)
            kst = cx.sb("kst", [128, 2, 64], F32)
            kstb = cx.sb("kstb", [128, 2, 64], BF16)
            PSB = [PS[6 + i][:].bitcast(BF16).rr("p (k t) -> p k t", k=8) for i in range(2)]
            SC = 192.0 ** -0.5
            it = 0
            for g in groups:
                T = g["T"]
                BLK = min(T, 512)
                nblk = T // BLK
                koff = PAST if g["ctx"] else 0
                NK = koff + T
                NKT = NK // 128
                for sidx in range(g["nseq"]):
                    ts0 = g["tok0"] + sidx * T
                    if g["ctx"]:
                        cx.dma("sp", cst[:], cckv[l, :, :].rr("(s p) c -> p s c", p=128))
                        cx.dma("sp", kst[:], ckpe[l, :, :].rr("(s p) c -> p s c", p=128))
                        cx.copy(cstb[:], cst[:])
                        cx.copy(kstb[:], kst[:])
                        pb = PSB[0]
                        for s in range(2):
                            for kc in range(2):
                                cx.tr(pb[:, s * 2 + kc, :], cstb[:, s, kc * 128:(kc + 1) * 128], ident_b[:])
                        for s in range(2):
                            for kc in range(2):
                                cx.copy(ckvnT[:, kc, s * 128:(s + 1) * 128], pb[:, s * 2 + kc, :])
                        pb = PSB[1]
                        for s in range(2):
                            cx.tr(pb[0:64, s, :], kstb[:, s, :], ident_b[:])
                        cx.copy(KPT[0:64, 0:256].rr("p (s t) -> p s t", s=2), pb[0:64, 0:2, :])
                    for bi in range(nblk):
                        t0 = ts0 + bi * BLK
                        k0 = koff + bi * BLK
                        i2 = it % 2
                        it += 1
                        raw, sq_, rs_ = rawT[i2], sqT[i2], rsm[i2]
                        cx.dma("sp", raw[:, 0:2, 0:BLK], projT_d[CH_CKV * 128:(CH_CKV + 2) * 128, t0:t0 + BLK].rr("(c p) t -> p c t", p=128))
                        cx.act(sq_[:, 0:2, 0:BLK], raw[:, 0:2, 0:BLK], AF.Square)
                        ps = PS[4 + i2]
                        for kc in range(2):
                            cx.mm(ps[:, 0:BLK], ones_b[:], sq_[:, kc, 0:BLK], start=(kc == 0), stop=(kc == 1))
                        cx.act(rs_[:, 0:BLK], ps[:, 0:BLK], AF.Ln, scale=1.0 / 256, bias=EPS)
                        cx.act(rs_[:, 0:BLK], rs_[:, 0:BLK], AF.Exp, scale=-0.5)
                        for kc in range(2):
                            cx.stt(ckvnT[:, kc, k0:k0 + BLK], raw[:, kc, 0:BLK], kvc[:, kc:kc + 1], rs_[:, 0:BLK], ALU.mult, ALU.mult)
                        if g["ctx"]:
                            kp_, ro_ = kpr[i2], ropet[i2]
                            cx.dma("sp", kp_[:, :, 0:BLK], projT_d[CH_KPE * 128:(CH_KPE + 1) * 128, t0:t0 + BLK].rr("(c p) t -> p c t", p=64))
                            cx.dma("sp", ro_[:, :, 0:BLK], rope_d[:, :, bi * BLK:(bi + 1) * BLK])
                            cx.tt(rt1[i2][:, 0:BLK], kp_[:, 0, 0:BLK], ro_[:, 0, 0:BLK], ALU.mult)
                            cx.tt(rt2[i2][:, 0:BLK], kp_[:, 1, 0:BLK], ro_[:, 1, 0:BLK], ALU.mult)
                            cx.tt(KPT[0:64, k0:k0 + BLK], rt1[i2][:, 0:BLK], rt2[i2][:, 0:BLK], ALU.add)
                        else:
                            cx.dma("sp", KPT[0:64, k0:k0 + BLK], projT_d[CH_KPE * 128:CH_KPE * 128 + 64, t0:t0 + BLK])
                    for kb in range(0, NK, 512):
                        n = min(512, NK - kb)
                        for h in range(4):
                            ps = PS[it % 4]
                            it += 1
                            for kc in range(2):
                                cx.mm(ps[:, 0:n], wukv[:, kc, h * 128:(h + 1) * 128], ckvnT[:, kc, kb:kb + n], start=(kc == 0), stop=(kc == 1))
                            cx.copy(KT[:, h, kb:kb + n], ps[:, 0:n], eng=("act" if it % 2 else "dve"))
                    for kt in range(NKT):
                        ps = PS[it % 4]
                        it += 1
                        for kc in range(2):
                            cx.mm(ps[:], ckvnT[:, kc, kt * 128:(kt + 1) * 128], wukv[:, kc, 512:1024], start=(kc == 0), stop=(kc == 1))
                        cx.copy(Vtm[:, kt, :], ps[:], eng=("act" if it % 2 else "dve"))
                    for bi in range(nblk):
                        t0 = ts0 + bi * BLK
                        i2 = it % 2
                        it += 1
                        raw, sq_, rs_, cq_, qt_ = rawT[i2], sqT[i2], rsm[i2], cqn[i2], QT[i2]
                        cx.dma("sp", raw[:, :, 0:BLK], projT_d[CH_CQ * 128:(CH_CQ + 3) * 128, t0:t0 + BLK].rr("(c p) t -> p c t", p=128))
                        cx.act(sq_[:, :, 0:BLK], raw[:, :, 0:BLK], AF.Square)
                        ps = PS[4 + i2]
                        for c in range(3):
                            cx.mm(ps[:, 0:BLK], ones_b[:], sq_[:, c, 0:BLK], start=(c == 0), stop=(c == 2))
                        cx.act(rs_[:, 0:BLK], ps[:, 0:BLK], AF.Ln, scale=1.0 / 384, bias=EPS)
                        cx.act(rs_[:, 0:BLK], rs_[:, 0:BLK], AF.Exp, scale=-0.5)
                        for c in range(3):
                            cx.stt(cq_[:, c, 0:BLK], raw[:, c, 0:BLK], qnc[:, c:c + 1], rs_[:, 0:BLK], ALU.mult, ALU.mult)
                        if g["ctx"]:
                            ro_ = ropet[i2]
                            cx.dma("sp", ro_[:, :, 0:BLK], rope_d[:, :, bi * BLK:(bi + 1) * BLK])
                        for h in range(4):
                            ps = PS[it % 4]
                            it += 1
                            for c in range(3):
                                cx.mm(ps[:, 0:BLK], wuq[:, c, h * 256:h * 256 + 128], cq_[:, c, 0:BLK], start=(c == 0), stop=(c == 2))
                            cx.copy(qt_[:, h, 0, 0:BLK], ps[:, 0:BLK], eng="act")
                            psa = PS[it % 4]
                            it += 1
                            for c in range(3):
                                cx.mm(psa[0:64, 0:BLK], wuq[:, c, h * 256 + 128:h * 256 + 192], cq_[:, c, 0:BLK], start=(c == 0), stop=(c == 2))
                            if g["ctx"]:
                                psb = PS[it % 4]
                                it += 1
                                for c in range(3):
                                    cx.mm(psb[0:64, 0:BLK], wuq[:, c, h * 256 + 192:h * 256 + 256], cq_[:, c, 0:BLK], start=(c == 0), stop=(c == 2))
                                cx.tt(rt1[i2][:, 0:BLK], psa[0:64, 0:BLK], ro_[:, 0, 0:BLK], ALU.mult)
                                cx.tt(rt2[i2][:, 0:BLK], psb[0:64, 0:BLK], ro_[:, 1, 0:BLK], ALU.mult)
                                cx.tt(qt_[0:64, h, 1, 0:BLK], rt1[i2][:, 0:BLK], rt2[i2][:, 0:BLK], ALU.add)
                            else:
                                cx.copy(qt_[0:64, h, 1, 0:BLK], psa[0:64, 0:BLK], eng="dve")
                        ob_ = obst[i2]
                        for h in range(4):
                            cx.dma("sp", zbt4[:, h, 0:BLK], projT_d[(CH_ZB + h) * 128:(CH_ZB + h + 1) * 128, t0:t0 + BLK])
                        for h in range(4):
                            cx.act(gzb4[:, h, 0:BLK], zbt4[:, h, 0:BLK], AF.Silu)
                        seqs = [(h, kt) for h in range(4) for kt in range(NKT)]

                        def emit_s(idx):
                            h, kt = seqs[idx]
                            pss = PS[idx % 4]
                            cx.mm(pss[:, 0:BLK], KT[:, h, kt * 128:(kt + 1) * 128], qt_[:, h, 0, 0:BLK], start=True, stop=False)
                            cx.mm(pss[:, 0:BLK], KPT[:, kt * 128:(kt + 1) * 128], qt_[:, h, 1, 0:BLK], start=False, stop=True)
                        emit_s(0)
                        for idx, (h, kt) in enumerate(seqs):
                            pso, psd = PS[4 + h % 2], PS[6 + h % 2]
                            if idx + 1 < len(seqs):
                                emit_s(idx + 1)
                            pt = PT[idx % 3]
                            cx.act(pt[:, 0:BLK], PS[idx % 4][:, 0:BLK], AF.Exp, scale=SC)
                            cx.mm(pso[:, 0:BLK], Vtm[:, kt, h * 128:(h + 1) * 128], pt[:, 0:BLK], start=(kt == 0), stop=(kt == NKT - 1))
                            cx.mm(psd[:, 0:BLK], ones_b[:], pt[:, 0:BLK], start=(kt == 0), stop=(kt == NKT - 1))
                            if kt == NKT - 1:
                                j2 = h % 2
                                cx.act(rden[j2][:, 0:BLK], psd[:, 0:BLK], AF.Ln)
                                cx.act(rden[j2][:, 0:BLK], rden[j2][:, 0:BLK], AF.Exp, scale=-1.0)
                                cx.tt(rden[j2][:, 0:BLK], rden[j2][:, 0:BLK], gzb4[:, h, 0:BLK], ALU.mult)
                                cx.tt(ob_[:, h, 0:BLK], pso[:, 0:BLK], rden[j2][:, 0:BLK], ALU.mult)
                        cx.dma("pool", mixT_d[1024:1536, t0:t0 + BLK].rr("(h p) t -> p h t", p=128), ob_[:, :, 0:BLK])
        if stop_after == "p3":
            continue
        cx.mark("L%d Phase 4" % l)
        with cx.scope():
            wpl = cx.sb("wpl", [128, 4, 128], BF16)
            for gi in range(4):
                cx.dma("pool", wpl[:, gi, :], w_pool[l, gi, :, :])
            psc = cx.sb("psc", [128, 4], F32)
            cx.dma("sp", psc[:], pscaleT[:, l, :])
            corr = cx.sb("corr", [128, 4, 16], F32)
            cx.dma("sp", corr[:].rr("p a b -> p (a b)"), corr_d[:, :].rr("a b -> (a b)").pbc())
            W = 528
            xct = [cx.sb("xct%d" % i, [128, W], BF16) for i in range(2)]
            xf = [cx.sb("xf%d" % i, [128, W], F32) for i in range(2)]
            aa = cx.sb("aa", [128, W], F32)
            bb_ = cx.sb("bb_", [128, W], F32)
            cc = cx.sb("cc", [128, W], F32)
            sm = cx.sb("sm", [128, 512], F32)
            pooled = [cx.sb("pooled%d" % i, [128, 512], BF16) for i in range(2)]
            zct = [cx.sb("zct%d" % i, [128, 512], BF16) for i in range(2)]
            gzc = [cx.sb("gzc%d" % i, [128, 512], F32) for i in range(2)]
            ocst = [cx.sb("ocst%d" % i, [128, 4, 512], BF16) for i in range(2)]
            it = 0
            for g in groups:
                T = g["T"]
                BLK = min(T, 512)
                nblk = T // BLK
                for sidx in range(g["nseq"]):
                    for bi in range(nblk):
                        t0 = g["tok0"] + sidx * T + bi * BLK
                        first, lastb = (bi == 0), (bi == nblk - 1)
                        oc_ = ocst[(it // 4) % 2]
                        for gi, win in enumerate((2, 4, 8, 16)):
                            i2 = it % 2
                            it += 1
                            x_, f_ = xct[i2], xf[i2]
                            lo = 8 if first else 0
                            hi = 8 if lastb else 0
                            if lo:
                                cx.memset(x_[:, 0:8], 0.0, eng="pool")
                            if hi:
                                cx.memset(x_[:, BLK + 8:BLK + 16], 0.0, eng="pool")
                            ch = CH_XC + gi
                            cx.dma("sp", x_[:, lo:BLK + 16 - hi], projT_d[ch * 128:(ch + 1) * 128, t0 - 8 + lo:t0 + BLK + 8 - hi])
                            cx.copy(f_[:, 0:BLK + 16], x_[:, 0:BLK + 16], eng="pool")
                            WW = BLK + 16
                            cx.tt(aa[:, 1:WW], f_[:, 0:WW - 1], f_[:, 1:WW], ALU.add)
                            if win == 2:
                                src = aa[:, 8:8 + BLK]
                            elif win == 4:
                                cx.tt(sm[:, 0:BLK], aa[:, 7:7 + BLK], aa[:, 9:9 + BLK], ALU.add)
                                src = sm[:, 0:BLK]
                            else:
                                cx.tt(bb_[:, 3:WW], aa[:, 1:WW - 2], aa[:, 3:WW], ALU.add)
                                if win == 8:
                                    cx.tt(sm[:, 0:BLK], bb_[:, 7:7 + BLK], bb_[:, 11:11 + BLK], ALU.add)
                                else:
                                    cx.tt(cc[:, 7:WW], bb_[:, 3:WW - 4], bb_[:, 7:WW], ALU.add)
                                    cx.tt(sm[:, 0:BLK], cc[:, 7:7 + BLK], cc[:, 15:15 + BLK], ALU.add)
                                src = sm[:, 0:BLK]
                            if first or lastb:
                                if win == 2:
                                    cx.copy(sm[:, 0:BLK], src)
                                    src = sm[:, 0:BLK]
                                if first:
                                    cx.tt(sm[:, 0:8], sm[:, 0:8], corr[:, gi, 0:8], ALU.mult)
                                if lastb:
                                    cx.tt(sm[:, BLK - 8:BLK], sm[:, BLK - 8:BLK], corr[:, gi, 8:16], ALU.mult)
                            pl = pooled[i2]
                            cx.stt(pl[:, 0:BLK], src, 1.0 / win, f_[:, 8:8 + BLK], ALU.mult, ALU.subtract)
                            ps = PS[it % 4]
                            cx.mm(ps[:, 0:BLK], wpl[:, gi, :], pl[:, 0:BLK])
                            cx.dma("sp", zct[i2][:, 0:BLK], projT_d[(CH_ZC + gi) * 128:(CH_ZC + gi + 1) * 128, t0:t0 + BLK])
                            cx.act(gzc[i2][:, 0:BLK], zct[i2][:, 0:BLK], AF.Silu)
                            cx.stt(oc_[:, gi, 0:BLK], ps[:, 0:BLK], psc[:, gi:gi + 1], gzc[i2][:, 0:BLK], ALU.mult, ALU.mult)
                        cx.dma("pool", mixT_d[1536:2048, t0:t0 + BLK].rr("(h p) t -> p h t", p=128), oc_[:, :, 0:BLK])
        if stop_after == "p4":
            continue
        cx.mark("L%d Phase 5" % l)
        with cx.scope():
            wo = cx.sb("wo", [128, 16, D], BF16)
            for k in range(16):
                cx.dma("pool", wo[:, k, :], w_out[l, k * 128:(k + 1) * 128, :])
            mt = [cx.sb("mt%d" % i, [128, 16, 128], BF16) for i in range(2)]
            xr = [cx.sb("xr%d" % i, [128, D], F32) for i in range(2)]
            yo = [cx.sb("yo%d" % i, [128, D], F32) for i in range(2)]
            junk5 = cx.sb("junk5", [128, 512], BF16)
            s5 = cx.sb("s5", [128, 8], F32)
            for g in groups:
                xin = (xp if g["name"] == "p" else xs) if l == 0 else (yp if g["name"] == "p" else ys)
                xout = yp if g["name"] == "p" else ys
                ntile = g["nseq"] * g["T"] // 128
                for ti in range(ntile):
                    i2 = ti % 2
                    tk = g["tok0"] + ti * 128
                    cx.dma("sp", mt[i2][:], mixT_d[:, tk:tk + 128].rr("(k p) t -> p k t", p=128))
                    cx.dma("sp", xr[i2][:], xin[ti * 128:(ti + 1) * 128, :])
                    for nb_ in range(4):
                        ps = PS[(ti % 2) * 4 + nb_]
                        for k in range(16):
                            cx.mm(ps[:], mt[i2][:, k, :], wo[:, k, nb_ * 512:(nb_ + 1) * 512], start=(k == 0), stop=(k == 15))
                        cx.act(junk5[:], ps[:], AF.Square, accum=s5[:, nb_:nb_ + 1])
                    cx.reduce(s5[:, 4:5], s5[:, 0:4], ALU.add)
                    cx.act(s5[:, 5:6], s5[:, 4:5], AF.Ln, scale=1.0 / D, bias=EPS)
                    cx.act(s5[:, 6:7], s5[:, 5:6], AF.Exp, scale=-0.5)
                    for nb_ in range(4):
                        ps = PS[(ti % 2) * 4 + nb_]
                        cs = slice(nb_ * 512, (nb_ + 1) * 512)
                        cx.stt(yo[i2][:, cs], ps[:], s5[:, 6:7], modbc[:, g["r"], 2, cs], ALU.mult, ALU.mult)
                    cx.tt(yo[i2][:], yo[i2][:], xr[i2][:], ALU.add, eng="pool")
                    cx.dma("pool", xout[ti * 128:(ti + 1) * 128, :], yo[i2][:])


_NC_CACHE = {}


def kernel(**inputs):
    from concourse.bass_utils import run_bass_kernel_spmd
    inp = {k: np.asarray(v) for k, v in inputs.items()}
    shared, per = host_prep(inp)
    if "nc" not in _NC_CACHE:
        _NC_CACHE["nc"] = build({})
    nc = _NC_CACHE["nc"]
    in_maps = [dict(shared, **per[c]) for c in range(NCORE)]
    res = run_bass_kernel_spmd(nc, in_maps, core_ids=list(range(NCORE)))
    rs = res.results
    y_prompt = np.concatenate([r["yp"].reshape(NP, TP, D) for r in rs], axis=0).astype(np.float32)
    y_sample = np.stack([r["ys"] for r in rs], axis=0).astype(np.float32)
    new_ckv = np.concatenate([r["nckv"] for r in rs], axis=0).astype(np.float32)
    new_kpe = np.concatenate([r["nkpe"] for r in rs], axis=0).astype(np.float32)
    new_sf = np.concatenate([r["nsf"] for r in rs], axis=0).astype(np.float32)
    new_sb = np.concatenate([r["nsb"] for r in rs], axis=0).astype(np.float32)
    return (y_prompt, y_sample, new_ckv, new_kpe, new_sf, new_sb)
```
